# Optimizing a Trainium2 kernel written in Bass

```python
import math
import jax, jax.numpy as jnp
from jax import lax
import numpy as np

D_MODEL = 2048
BATCH = 2
SEQ = 4096
DEPTH = 1
DEC_BATCH = 1
DEC_SEQ = 8192
PAST_LEN = 128

PLE_DIM = 256
CONV_WIDTH = 2048
CONV_KERNEL = 31
N_HEADS = 16
QK_NOPE = 128
QK_ROPE = 64
V_HEAD = 128
Q_LORA = 512
KV_LORA = 512
ATTN_WIDTH = N_HEADS * V_HEAD
ROPE_THETA = 10000.0
Q_BLOCK = 128
EPS = 1e-6
ATTN_SCALE = 1.0 / math.sqrt(QK_NOPE + QK_ROPE)
IN_SIZES = (CONV_WIDTH, CONV_WIDTH, CONV_WIDTH, Q_LORA, KV_LORA, QK_ROPE, ATTN_WIDTH, D_MODEL, D_MODEL)
IN_WIDTH = sum(IN_SIZES)

kernel_name = "hybrid_conv_mla_gated_encoder"


def rmsnorm(x, g):
    xf = x.astype(jnp.float32)
    y = xf * lax.rsqrt(jnp.mean(xf * xf, axis=-1, keepdims=True) + EPS)
    return (y * g.astype(jnp.float32)).astype(x.dtype)


def layernorm(x, g, b):
    xf = x.astype(jnp.float32)
    mu = jnp.mean(xf, axis=-1, keepdims=True)
    var = jnp.mean(jnp.square(xf - mu), axis=-1, keepdims=True)
    y = (xf - mu) * lax.rsqrt(var + EPS)
    return (y * g.astype(jnp.float32) + b.astype(jnp.float32)).astype(x.dtype)


def rope_tables(seq_len, dtype):
    inv = 1.0 / (ROPE_THETA ** (jnp.arange(0, QK_ROPE, 2, dtype=jnp.float32) / QK_ROPE))
    ang = jnp.arange(seq_len, dtype=jnp.float32)[:, None] * inv[None, :]
    return jnp.cos(ang).astype(dtype), jnp.sin(ang).astype(dtype)


def apply_rope(x, cos, sin):
    x1, x2 = jnp.split(x, 2, axis=-1)
    return jnp.concatenate([x1 * cos - x2 * sin, x1 * sin + x2 * cos], axis=-1)


def split_points():
    return [int(v) for v in np.cumsum(IN_SIZES)[:-1]]


def mla_attend(q_nope, q_rope, k_nope, k_rope, v):
    B, S, H, _ = q_nope.shape
    nb = S // Q_BLOCK
    qn = q_nope.reshape(B, nb, Q_BLOCK, H, QK_NOPE).transpose(1, 0, 2, 3, 4)
    qr = q_rope.reshape(B, nb, Q_BLOCK, H, QK_ROPE).transpose(1, 0, 2, 3, 4)

    def block(args):
        qn_b, qr_b = args
        s = (jnp.einsum('bqhd,bkhd->bhqk', qn_b, k_nope)
             + jnp.einsum('bqhr,bkr->bhqk', qr_b, k_rope)).astype(jnp.float32) * ATTN_SCALE
        pr = jax.nn.softmax(s, axis=-1).astype(v.dtype)
        return jnp.einsum('bhqk,bkhd->bqhd', pr, v)

    o = lax.map(block, (qn, qr))
    return o.transpose(1, 0, 2, 3, 4).reshape(B, S, H * V_HEAD)


def encoder_layer(x, p, g_pre, w_in, q_norm, w_uq, kv_norm, w_ukv, conv_w, conv_b,
                  ln_g, ln_b, w_conv_out, w_o, w_out, w_pe, w_pg, cos, sin):
    B, S, _ = x.shape
    h = rmsnorm(x, g_pre)
    u = h @ w_in
    cv, cg, zc, cq, ckv, kr, za, gc, ga = jnp.split(u, split_points(), axis=-1)

    a = cv * jax.nn.sigmoid(cg)
    a = lax.conv_general_dilated(
        a, conv_w, window_strides=(1,), padding=[(CONV_KERNEL // 2, CONV_KERNEL // 2)],
        dimension_numbers=('NWC', 'WIO', 'NWC'), feature_group_count=CONV_WIDTH) + conv_b
    a = jax.nn.silu(layernorm(a, ln_g, ln_b)) * jax.nn.silu(zc)
    b_conv = a @ w_conv_out

    q = (rmsnorm(cq, q_norm) @ w_uq).reshape(B, S, N_HEADS, QK_NOPE + QK_ROPE)
    q_nope, q_rope = q[..., :QK_NOPE], q[..., QK_NOPE:]
    q_rope = apply_rope(q_rope, cos[:, None, :], sin[:, None, :])
    kv = (rmsnorm(ckv, kv_norm) @ w_ukv).reshape(B, S, N_HEADS, QK_NOPE + V_HEAD)
    k_nope, v = kv[..., :QK_NOPE], kv[..., QK_NOPE:]
    k_rope = apply_rope(kr, cos, sin)
    o = mla_attend(q_nope, q_rope, k_nope, k_rope, v)
    b_attn = (o * jax.nn.silu(za)) @ w_o

    m = jax.nn.sigmoid(gc) * b_conv + jax.nn.sigmoid(ga) * b_attn
    x = x + m @ w_out

    x = x + (p @ w_pe) * jax.nn.sigmoid(x @ w_pg)
    return x


def trunk(x, p_all, g_pre, w_in, q_norm, w_uq, kv_norm, w_ukv, conv_w, conv_b,
          ln_g, ln_b, w_conv_out, w_o, w_out, w_pe, w_pg, g_final):
    cos, sin = rope_tables(x.shape[1], x.dtype)
    for i in range(DEPTH):
        x = encoder_layer(x, p_all[i], g_pre[i], w_in[i], q_norm[i], w_uq[i], kv_norm[i],
                          w_ukv[i], conv_w[i], conv_b[i], ln_g[i], ln_b[i], w_conv_out[i],
                          w_o[i], w_out[i], w_pe[i], w_pg[i], cos, sin)
    return rmsnorm(x, g_final)


def setup_inputs(seed: int = 0) -> dict:
    key = jax.random.key(seed)
    ks = jax.random.split(key, 24)
    f32 = jnp.float32

    def nrm(k, shape, scale):
        return jax.random.normal(k, shape, f32) * scale

    def gain(k, shape):
        return 1.0 + 0.05 * jax.random.normal(k, shape, f32)

    return {
        "x_prompt": nrm(ks[0], (BATCH, SEQ, D_MODEL), 1.0),
        "x_sample": nrm(ks[1], (DEC_BATCH, DEC_SEQ, D_MODEL), 1.0),
        "p_prompt": nrm(ks[2], (DEPTH, BATCH, SEQ, PLE_DIM), 1.0),
        "p_sample": nrm(ks[3], (DEPTH, DEC_BATCH, DEC_SEQ, PLE_DIM), 1.0),
        "g_pre": gain(ks[4], (DEPTH, D_MODEL)),
        "w_in": nrm(ks[5], (DEPTH, D_MODEL, IN_WIDTH), D_MODEL ** -0.5),
        "q_norm": gain(ks[6], (DEPTH, Q_LORA)),
        "w_uq": nrm(ks[7], (DEPTH, Q_LORA, N_HEADS * (QK_NOPE + QK_ROPE)), Q_LORA ** -0.5),
        "kv_norm": gain(ks[8], (DEPTH, KV_LORA)),
        "w_ukv": nrm(ks[9], (DEPTH, KV_LORA, N_HEADS * (QK_NOPE + V_HEAD)), KV_LORA ** -0.5),
        "conv_w": nrm(ks[10], (DEPTH, CONV_KERNEL, 1, CONV_WIDTH), CONV_KERNEL ** -0.5),
        "conv_b": nrm(ks[11], (DEPTH, CONV_WIDTH), 0.02),
        "ln_g": gain(ks[12], (DEPTH, CONV_WIDTH)),
        "ln_b": nrm(ks[13], (DEPTH, CONV_WIDTH), 0.02),
        "w_conv_out": nrm(ks[14], (DEPTH, CONV_WIDTH, D_MODEL), CONV_WIDTH ** -0.5),
        "w_o": nrm(ks[15], (DEPTH, ATTN_WIDTH, D_MODEL), ATTN_WIDTH ** -0.5),
        "w_out": nrm(ks[16], (DEPTH, D_MODEL, D_MODEL), D_MODEL ** -0.5),
        "w_pe": nrm(ks[17], (DEPTH, PLE_DIM, D_MODEL), PLE_DIM ** -0.5),
        "w_pg": nrm(ks[18], (DEPTH, D_MODEL, D_MODEL), D_MODEL ** -0.5),
        "g_final": gain(ks[19], (D_MODEL,)),
    }


def reference(x_prompt, x_sample, p_prompt, p_sample, g_pre, w_in, q_norm, w_uq, kv_norm,
              w_ukv, conv_w, conv_b, ln_g, ln_b, w_conv_out, w_o, w_out, w_pe, w_pg, g_final):
    y_prompt = trunk(x_prompt, p_prompt, g_pre, w_in, q_norm, w_uq, kv_norm, w_ukv, conv_w,
                     conv_b, ln_g, ln_b, w_conv_out, w_o, w_out, w_pe, w_pg, g_final)
    y_sample = trunk(x_sample, p_sample, g_pre, w_in, q_norm, w_uq, kv_norm, w_ukv, conv_w,
                     conv_b, ln_g, ln_b, w_conv_out, w_o, w_out, w_pe, w_pg, g_final)
    return (y_prompt, y_sample)
```

```python
import math
from contextlib import ExitStack

import numpy as np
import concourse.bass as bass
import concourse.mybir as mybir
from concourse.bass_utils import run_bass_kernel_spmd

F32 = mybir.dt.float32
BF16 = mybir.dt.bfloat16
ALU = mybir.AluOpType
AF = mybir.ActivationFunctionType

D = 2048
KC = 16
TOK = 512
NH = 16
EPS = 1e-6
ATTN_SCALE = 1.0 / math.sqrt(192.0)
CONVK = 31
NCORES = 8

V_GFIN, V_QN, V_KVN, V_CB, V_LNG, V_LNB, V_CW = 0, 16, 20, 24, 40, 56, 72
NV = 72 + 16 * CONVK

ENGS = ("pe", "act", "dve", "pool", "sp")
NSLOT = 8


class Op:
    __slots__ = ("eng", "insts", "dma", "deps", "need_inc", "sem", "val", "waits")

    def __init__(self, eng, insts, dma):
        self.eng, self.insts, self.dma = eng, insts, dma
        self.deps = set()
        self.need_inc = dma
        self.sem = None
        self.val = 0
        self.waits = []


def I(name, *a, **kw):
    return (name, a, kw)


class Prog:
    def __init__(self):
        self.by_eng = {e: [] for e in ENGS}
        self.last_write = {}
        self.reads = {}
        self.last_real = {}
        self.dma_pending = []

    def op(self, eng, insts, reads=(), writes=(), dma=False):
        if isinstance(insts, tuple):
            insts = [insts]
        o = Op(eng, insts, dma)
        deps = []
        for k in reads:
            w = self.last_write.get(k)
            if w is not None:
                deps.append((w, 0))
        for k in writes:
            w = self.last_write.get(k)
            if w is not None:
                deps.append((w, 1))
            r = self.reads.get(k)
            if r:
                for x in r[0].values():
                    deps.append((x, 2))
                for x in r[1]:
                    deps.append((x, 2))
        for p, kind in deps:
            if p.eng == eng and not p.dma and not dma:
                if eng == "pe" or kind != 0:
                    continue
            o.deps.add(p)
        for k in reads:
            r = self.reads.setdefault(k, ({}, []))
            if dma:
                r[1].append(o)
            else:
                r[0][eng] = o
        for k in writes:
            self.last_write[k] = o
            self.reads[k] = ({}, [])
        self.by_eng[eng].append(o)
        self.last_real[eng] = o
        if dma:
            self.dma_pending.append(o)
        return o

    def dma(self, eng, out, in_, reads=(), writes=()):
        return self.op(eng, I("dma_start", out=out, in_=in_), reads, writes, dma=True)

    def barrier(self):
        targets = list(self.last_real.values()) + self.dma_pending
        for e in ENGS:
            o = Op(e, None, False)
            o.need_inc = False
            for t in targets:
                if t.eng == e and not t.dma:
                    continue
                o.deps.add(t)
            self.by_eng[e].append(o)
        self.dma_pending = []
        self.last_write = {}
        self.reads = {}

    def finalize(self, csem, slotsem):
        for e in ENGS:
            for o in self.by_eng[e]:
                for p in o.deps:
                    p.need_inc = True
        for e in ENGS:
            cnt = 0
            nd = 0
            for o in self.by_eng[e]:
                if o.insts is None:
                    continue
                if o.dma:
                    o.sem = slotsem[e][nd % NSLOT]
                    o.val = 16 * (nd // NSLOT + 1)
                    nd += 1
                elif o.need_inc:
                    cnt += 1
                    o.sem = csem[e]
                    o.val = cnt
        nwait = 0
        for e in ENGS:
            known = {}
            for o in self.by_eng[e]:
                need = {}
                for p in o.deps:
                    k = id(p.sem)
                    if k not in need or need[k][1] < p.val:
                        need[k] = (p.sem, p.val)
                if o.dma and o.val > 16:
                    k = id(o.sem)
                    v = o.val - 16
                    if k not in need or need[k][1] < v:
                        need[k] = (o.sem, v)
                for k, (s, v) in need.items():
                    if known.get(k, 0) >= v:
                        continue
                    known[k] = v
                    o.waits.append((s, v))
                    nwait += 1
        self.stats = {e: len(self.by_eng[e]) for e in ENGS}
        self.stats["waits"] = nwait

    def emit(self, e, eng):
        for o in self.by_eng[e]:
            for s, v in o.waits:
                eng.wait_ge(s, v)
            if o.insts is not None:
                inst = None
                for name, a, kw in o.insts:
                    inst = getattr(eng, name)(*a, **kw)
                if o.need_inc:
                    inst.then_inc(o.sem, 16 if o.dma else 1)


class Arena:
    def __init__(self, t, nwords):
        self.t = t
        self.nwords = nwords
        self.off = 0
        self.peak = 0

    def alloc(self, nelem, dt):
        nb = nelem * (2 if dt == BF16 else 4)
        nw = (nb + 63) // 64 * 16
        assert self.off + nw <= self.nwords, f"SBUF arena overflow {self.off + nw} > {self.nwords}"
        ap = self.t[:, self.off:self.off + nw]
        self.off += nw
        self.peak = max(self.peak, self.off)
        if dt == BF16:
            ap = ap.bitcast(BF16)
        return ap[:, 0:nelem]


class Cfg:
    def __init__(self, nt=2, s0=8192, s1=4096):
        self.NT = nt
        self.S = (s0, s1)
        self.OWN = nt * TOK


def build_program(cfg):
    NT = cfg.NT
    OWN = cfg.OWN
    nc = bass.Bass("TRN2", target_bir_lowering=False)

    def din(name, shape, dt=F32):
        return nc.dram_tensor(name, list(shape), dt, kind="ExternalInput").ap()

    xseq = [din(f"xseq{s}", (cfg.S[s], D)) for s in range(2)]
    cosd = [din(f"cos{s}", (64, cfg.S[s])) for s in range(2)]
    sind = [din(f"sin{s}", (64, cfg.S[s])) for s in range(2)]
    xhalo = din("xhalo", (2, NT, 32, D))
    pown = din("pown", (2, OWN, 256))
    gpre_d = din("gpre_b", (128, D))
    vecs_d = din("vecs", (128, NV))
    ident_d = din("ident", (128, 128))
    wlat_d = din("w_lat", (128, KC * 640))
    wcq_d = din("w_cq", (4, 128, KC * 128))
    wuk_d = din("w_uk", (128, 4 * 2048))
    wuv_d = din("w_uv", (128, 4 * 2048))
    wuq_d = din("w_uq", (NH, 128, 4 * 256))
    wA_d = din("wA", (160, 128, KC * 128))
    wpe_d = din("w_pe", (16, 128, 2 * 128))
    yout = [nc.dram_tensor(f"y{s}", [OWN, D], F32, kind="ExternalOutput").ap() for s in range(2)]
    Kd = [nc.dram_tensor(f"kscr{s}", [NH, 128, cfg.S[s]], BF16, kind="Internal").ap() for s in range(2)]
    Vd = [nc.dram_tensor(f"vscr{s}", [cfg.S[s] // 128, 128, D], BF16, kind="Internal").ap() for s in range(2)]

    wAbf_d = nc.dram_tensor("wA_bf", [160, 128, KC * 128], BF16, kind="Internal").ap()

    P = Prog()
    es = ExitStack()
    NW = 53000
    arena_t = es.enter_context(nc.sbuf_tensor("arena", [128, NW], F32))
    ps_t = es.enter_context(nc.psum_tensor("ps", [128, 8, 512], F32))
    AR = Arena(arena_t, NW)

    def bank(b):
        return ps_t[:, b, :]

    bank_ctr = [0]
    reserved = set()

    def next_bank(n=1):
        while True:
            b = bank_ctr[0]
            if b + n > 8:
                b = 0
            bank_ctr[0] = (b + n) % 8
            if all((b + j) not in reserved for j in range(n)):
                return b

    ident_f = AR.alloc(128, F32)
    ident_b = AR.alloc(128, BF16)
    ones_b = AR.alloc(128, BF16)
    vecs = AR.alloc(NV, F32)
    gpre = AR.alloc(D, F32)
    kvn_s = AR.alloc(4, F32)
    qn_s = AR.alloc(4, F32)
    gfin_s = AR.alloc(16, F32)
    cst = AR.alloc(4, F32)
    C_DEPS, C_512EPS, C_EPS = cst[:, 0:1], cst[:, 1:2], cst[:, 2:3]
    P.op("dve", I("memset", C_DEPS, D * EPS), writes=["cst"])
    P.op("dve", I("memset", C_512EPS, 512 * EPS), writes=["cst"])
    P.op("dve", I("memset", C_EPS, EPS), writes=["cst"])

    def rsqrt_op(out, in_, cbias, reads, key):
        P.op("act", I("activation", out, in_, AF.Sqrt, bias=cbias, scale=1.0), reads=reads, writes=[key])
        P.op("dve", I("reciprocal", out, out), reads=[key], writes=[key])

    P.dma("sp", ident_f, ident_d, writes=["ident_f"])
    P.dma("pool", ident_b, ident_d, writes=["ident_b"])
    P.dma("sp", vecs, vecs_d, writes=["vecs"])
    P.dma("sp", gpre, gpre_d, writes=["gpre"])
    P.op("dve", I("memset", ones_b, 1.0), writes=["ones"])
    P.op("dve", I("tensor_scalar", gpre, gpre, math.sqrt(D), None, ALU.mult), reads=["gpre"], writes=["gpre"])
    P.op("dve", I("tensor_scalar", kvn_s, vecs[:, V_KVN:V_KVN + 4], math.sqrt(512.0), None, ALU.mult),
         reads=["vecs"], writes=["kvn_s"])
    P.op("dve", I("tensor_scalar", qn_s, vecs[:, V_QN:V_QN + 4], math.sqrt(512.0), None, ALU.mult),
         reads=["vecs"], writes=["qn_s"])
    P.op("dve", I("tensor_scalar", gfin_s, vecs[:, V_GFIN:V_GFIN + 16], math.sqrt(D), None, ALU.mult),
         reads=["vecs"], writes=["gfin_s"])
    P.barrier()

    evac_rr = [0]

    def evac_copy(out, in_, reads, writes):
        evac_rr[0] ^= 1
        if evac_rr[0]:
            return P.op("act", I("activation", out, in_, AF.Copy), reads=reads, writes=writes)
        return P.op("dve", I("tensor_copy", out, in_), reads=reads, writes=writes)

    def mm_group(out, pairs, reads, writes):
        n = len(pairs)
        insts = [I("matmul", out, l, r, start=(i == 0), stop=(i == n - 1)) for i, (l, r) in enumerate(pairs)]
        return P.op("pe", insts, reads=reads, writes=writes)

    def rms_tokens(xb, hsb, ssb, rows, keyx, keyh):
        P.op("act", I("activation", hsb[0:rows, :], xb[0:rows, :], AF.Square, accum_out=ssb[0:rows, :]),
             reads=[keyx], writes=[keyh, keyh + "ss"])
        rsqrt_op(ssb[0:rows, :], ssb[0:rows, :], C_DEPS[0:rows, :], [keyh + "ss"], keyh + "ss")
        P.op("dve", I("scalar_tensor_tensor", hsb[0:rows, :], xb[0:rows, :], ssb[0:rows, 0:1], gpre[0:rows, :],
                      ALU.mult, ALU.mult),
             reads=[keyx, keyh + "ss", "gpre"], writes=[keyh])

    def transpose_to_hT(hsb, rows, keyh, hT3, col0, keyout):
        b = next_bank(2)
        pv = ps_t[:, b:b + 2, :].bitcast(BF16).rearrange("p b (c t) -> p (b c) t", t=128)
        insts = [I("transpose", pv[:, c, 0:rows], hsb[0:rows, c * 128:(c + 1) * 128], ident_b[0:rows, 0:rows])
                 for c in range(KC)]
        P.op("pe", insts, reads=[keyh, "ident_b"], writes=[("ps", b), ("ps", b + 1)])
        evac_copy(hT3[:, :, col0:col0 + rows], pv[:, :, 0:rows], [("ps", b), ("ps", b + 1)], [keyout])

    featnorm_rstd = AR.alloc(TOK, F32)
    rope_t1 = AR.alloc(TOK, F32)[0:64, :]
    rope_t2 = AR.alloc(TOK, F32)[0:64, :]
    cqn = AR.alloc(4 * OWN, BF16).rearrange("p (c t) -> p c t", c=4)
    krope = AR.alloc(max(cfg.S), BF16)
    P.op("dve", I("memset", krope[64:128, :], 0.0), writes=["krope_pad"])
    P.barrier()
    persist_off = AR.off

    def featnorm(srcbanks, nchunk, gain_s, out3, col0, tag, sqring):
        sb = next_bank(1)
        reserved.add(sb)
        for c in range(nchunk):
            sq = sqring[c % len(sqring)]
            kq = ("fnsq", c % len(sqring))
            P.op("act", I("activation", sq, bank(srcbanks[c]), AF.Square), reads=[("ps", srcbanks[c])], writes=[kq])
            P.op("pe", I("matmul", bank(sb), ones_b, sq, start=(c == 0), stop=(c == nchunk - 1)),
                 reads=[kq, "ones"], writes=[("ps", sb)])
        reserved.discard(sb)
        rb = featnorm_rstd
        assert nchunk == 4
        rsqrt_op(rb, bank(sb), C_512EPS, [("ps", sb)], "fn_rstd")
        for c in range(nchunk):
            P.op("dve", I("scalar_tensor_tensor", out3[:, c, col0:col0 + TOK], bank(srcbanks[c]), gain_s[:, c:c + 1], rb,
                          ALU.mult, ALU.mult),
                 reads=[("ps", srcbanks[c]), "fn_rstd"], writes=[(tag + "out", c)])

    def rope(pb_a, pb_b, cosT, sinT, out, reads, writes):
        P.op("dve", I("tensor_tensor", rope_t1, bank(pb_a)[0:64, :], cosT, ALU.mult),
             reads=[("ps", pb_a)] + reads, writes=["rope_t1"])
        P.op("dve", I("tensor_tensor", rope_t2, bank(pb_b)[0:64, :], sinT, ALU.mult),
             reads=[("ps", pb_b)] + reads, writes=["rope_t2"])
        P.op("dve", I("tensor_tensor", out, rope_t1, rope_t2, ALU.add),
             reads=["rope_t1", "rope_t2"], writes=writes)

    def phase1(seg):
        S = cfg.S[seg]
        NTL = S // TOK
        AR.off = persist_off
        wlat = AR.alloc(KC * 640, BF16).rearrange("p (c n) -> p c n", c=KC)
        wuk = AR.alloc(4 * 2048, BF16).rearrange("p (c n) -> p c n", c=4)
        wuv = AR.alloc(4 * 2048, BF16).rearrange("p (c n) -> p c n", c=4)
        xb = [AR.alloc(D, F32) for _ in range(2)]
        hsb = [AR.alloc(D, BF16) for _ in range(4)]
        ssb = [AR.alloc(1, F32) for _ in range(4)]
        hTs = [AR.alloc(KC * TOK, BF16).rearrange("p (c t) -> p c t", c=KC) for _ in range(2)]
        ckvns = [AR.alloc(4 * TOK, BF16).rearrange("p (c t) -> p c t", c=4) for _ in range(2)]
        sqring = [AR.alloc(TOK, BF16) for _ in range(2)]
        cosTs = [AR.alloc(TOK, F32)[0:64, :] for _ in range(2)]
        sinTs = [AR.alloc(TOK, F32)[0:64, :] for _ in range(2)]
        kst = [AR.alloc(4 * TOK, BF16).rearrange("p (h t) -> p h t", h=4) for _ in range(2)]
        vst = [AR.alloc(D, BF16) for _ in range(2)]
        wcq = [AR.alloc(KC * 128, BF16).rearrange("p (c n) -> p c n", c=KC) for _ in range(2)]

        P.dma("pool", wlat.rearrange("p c n -> p (c n)"), wlat_d, writes=["wlat"])
        P.dma("pool", wuk.rearrange("p c n -> p (c n)"), wuk_d, writes=["wuk"])
        P.dma("pool", wuv.rearrange("p c n -> p (c n)"), wuv_d, writes=["wuv"])

        def front_a(tt):
            par = tt % 2
            t0 = tt * TOK
            P.dma("sp", cosTs[par], cosd[seg][:, t0:t0 + TOK], writes=[("cosT", par)])
            P.dma("sp", sinTs[par], sind[seg][:, t0:t0 + TOK], writes=[("sinT", par)])
            for sub in range(4):
                i = sub % 2
                r0 = t0 + sub * 128
                P.dma("sp", xb[i], xseq[seg][r0:r0 + 128, :], writes=[f"xb{i}"])
                rms_tokens(xb[i], hsb[sub], ssb[sub], 128, f"xb{i}", f"hs{sub}")

        def front_b(tt):
            par = tt % 2
            for sub in range(4):
                transpose_to_hT(hsb[sub], 128, f"hs{sub}", hTs[par], sub * 128, ("hT", par, sub))

        def mid(tt):
            par = tt % 2
            t0 = tt * TOK
            hT = hTs[par]
            ckvn = ckvns[par]
            hkeys = [("hT", par, s_) for s_ in range(4)]
            cb = [next_bank(1) for _ in range(4)]
            for c in range(4):
                mm_group(bank(cb[c]), [(wlat[:, k, c * 128:(c + 1) * 128], hT[:, k, :]) for k in range(KC)],
                         hkeys + ["wlat"], [("ps", cb[c])])
            kb = next_bank(1)
            mm_group(bank(kb)[0:64, :], [(wlat[:, k, 512:576], hT[:, k, :]) for k in range(KC)], hkeys + ["wlat"], [("ps", kb)])
            kb2 = next_bank(1)
            mm_group(bank(kb2)[0:64, :], [(wlat[:, k, 576:640], hT[:, k, :]) for k in range(KC)], hkeys + ["wlat"], [("ps", kb2)])
            featnorm(cb, 4, kvn_s, ckvn, 0, f"kvn{par}", sqring)
            rope(kb, kb2, cosTs[par], sinTs[par], krope[0:64, t0:t0 + TOK], [("cosT", par), ("sinT", par)], [("krope", tt)])
            if tt < NT:
                qb = []
                for c in range(4):
                    w = wcq[c % 2]
                    P.dma("pool", w.rearrange("p c n -> p (c n)"), wcq_d[c], writes=[("wcq", c % 2)])
                    b = next_bank(1)
                    qb.append(b)
                    mm_group(bank(b), [(w[:, k, :], hT[:, k, :]) for k in range(KC)], hkeys + [("wcq", c % 2)], [("ps", b)])
                featnorm(qb, 4, qn_s, cqn, t0, "qn", sqring)

        def back_k(tt):
            par = tt % 2
            t0 = tt * TOK
            ckvn = ckvns[par]
            ck = [(f"kvn{par}out", c) for c in range(4)]
            for hg in range(4):
                ks = kst[hg % 2]
                for hh in range(4):
                    h = hg * 4 + hh
                    b = next_bank(1)
                    mm_group(bank(b), [(wuk[:, c, h * 128:(h + 1) * 128], ckvn[:, c, :]) for c in range(4)],
                             ck + ["wuk"], [("ps", b)])
                    evac_copy(ks[:, hh, :], bank(b), [("ps", b)], [("kst", hg % 2, hh)])
                P.dma("sp", Kd[seg][hg * 4:(hg + 1) * 4, :, t0:t0 + TOK].rearrange("h p t -> p h t"), ks,
                      reads=[("kst", hg % 2, x) for x in range(4)], writes=[("Kd", seg, hg, tt)])

        def back_v(tt):
            par = tt % 2
            ckvn = ckvns[par]
            ck = [(f"kvn{par}out", c) for c in range(4)]
            for sub in range(4):
                vs = vst[sub % 2]
                for g in range(4):
                    b = next_bank(1)
                    mm_group(bank(b), [(ckvn[:, c, sub * 128:(sub + 1) * 128], wuv[:, c, g * 512:(g + 1) * 512])
                                       for c in range(4)], ck + ["wuv"], [("ps", b)])
                    evac_copy(vs[:, g * 512:(g + 1) * 512], bank(b), [("ps", b)], [("vst", sub % 2, g)])
                P.dma("sp", Vd[seg][tt * 4 + sub], vs,
                      reads=[("vst", sub % 2, x) for x in range(4)], writes=[("Vd", seg, tt, sub)])

        front_a(0)
        front_b(0)
        for tt in range(NTL):
            if tt + 1 < NTL:
                front_a(tt + 1)
            mid(tt)
            if tt >= 1:
                back_k(tt - 1)
            if tt + 1 < NTL:
                front_b(tt + 1)
            if tt >= 1:
                back_v(tt - 1)
        back_k(NTL - 1)
        back_v(NTL - 1)

    def phase2(seg, tile, og):
        S = cfg.S[seg]
        NKT = S // 128
        NP = NKT // 2
        assert NP % 2 == 0 and NP >= 4
        q0 = tile * TOK
        mark = AR.off
        Kb = [AR.alloc(S, BF16) for _ in range(2)]
        Vb = [AR.alloc(S, BF16).rearrange("p (k d) -> p k d", d=128) for _ in range(2)]
        wq = [AR.alloc(4 * 256, BF16).rearrange("p (c n) -> p c n", c=4) for _ in range(2)]
        qn = [AR.alloc(TOK, BF16) for _ in range(2)]
        qr = [AR.alloc(TOK, BF16) for _ in range(2)]
        for qq in range(2):
            P.op("dve", I("memset", qr[qq][64:128, :], 0.0), writes=[("qrpad", qq)])
        NPT = 4
        PT = [AR.alloc(2 * TOK, BF16).rearrange("p (b t) -> p b t", b=2) for _ in range(NPT)]
        s1 = [AR.alloc(TOK, BF16) for _ in range(2)]
        s2 = [AR.alloc(TOK, BF16) for _ in range(2)]
        osb = AR.alloc(TOK, F32)
        dsb = AR.alloc(TOK, F32)
        cosq = AR.alloc(TOK, F32)[0:64, :]
        sinq = AR.alloc(TOK, F32)[0:64, :]
        SCP, OBK, DBK, QA, QB = ((0, 1), (2, 3)), 4, 5, 6, 7

        P.dma("sp", cosq, cosd[seg][:, q0:q0 + TOK], writes=["cosq"])
        P.dma("sp", sinq, sind[seg][:, q0:q0 + TOK], writes=["sinq"])
        VSTEP = 16
        cqk = [("qnout", c) for c in range(4)]

        def load_head(h):
            i = h % 2
            P.dma("pool", wq[i].rearrange("p c n -> p (c n)"), wuq_d[h], writes=[("wq", i)])
            P.dma("sp", Kb[i], Kd[seg][h], writes=[("Kb", i)])
            for k0 in range(0, NKT, VSTEP):
                P.dma("sp", Vb[i][:, k0:k0 + VSTEP, :],
                      Vd[seg][k0:k0 + VSTEP, :, h * 128:(h + 1) * 128].rearrange("k p d -> p k d"),
                      writes=[("Vb", i, k0)])

        def qproj(h):
            i = h % 2
            mm_group(bank(QA), [(wq[i][:, c, 0:128], cqn[:, c, q0:q0 + TOK]) for c in range(4)],
                     [("wq", i)] + cqk, [("ps", QA)])
            P.op("dve", I("tensor_copy", qn[i], bank(QA)), reads=[("ps", QA)], writes=[("qn", i)])
            mm_group(bank(QB)[0:64, :], [(wq[i][:, c, 128:192], cqn[:, c, q0:q0 + TOK]) for c in range(4)],
                     [("wq", i)] + cqk, [("ps", QB)])
            mm_group(bank(QA)[0:64, :], [(wq[i][:, c, 192:256], cqn[:, c, q0:q0 + TOK]) for c in range(4)],
                     [("wq", i)] + cqk, [("ps", QA)])
            rope(QB, QA, cosq, sinq, qr[i][0:64, :], ["cosq", "sinq"], [("qr", i)])

        load_head(0)
        qproj(0)
        for h in range(NH):
            i = h % 2
            if h + 1 < NH:
                load_head(h + 1)
            if seg == 0:
                per = 160 // NT
                ph = (per + NH - 1) // NH
                for blk in range(tile * per + h * ph, min((tile + 1) * per, tile * per + (h + 1) * ph)):
                    P.dma("pool", wAbf_d[blk], wA_d[blk], writes=[("wAbf", blk)])
            vkeys = [("Vb", i, k0) for k0 in range(0, NKT, VSTEP)]

            def scores(p, i=i):
                b0, b1 = SCP[p % 2]
                insts = []
                for b, kt in ((b0, 2 * p), (b1, 2 * p + 1)):
                    insts.append(I("matmul", bank(b), Kb[i][:, kt * 128:(kt + 1) * 128], qn[i], start=True, stop=False))
                    insts.append(I("matmul", bank(b), krope[:, kt * 128:(kt + 1) * 128], qr[i], start=False, stop=True))
                P.op("pe", insts, reads=[("Kb", i), ("qn", i), ("qr", i), ("qrpad", i)], writes=[("ps", b0), ("ps", b1)])

            def pv(p, i=i, vkeys=vkeys):
                pt = PT[p % NPT]
                P.op("pe", [I("matmul", bank(OBK), Vb[i][:, 2 * p, :], pt[:, 0, :], start=(p == 0), stop=False),
                            I("matmul", bank(OBK), Vb[i][:, 2 * p + 1, :], pt[:, 1, :], start=False, stop=(p == NP - 1))],
                     reads=[("PT", p % NPT)] + vkeys, writes=[("ps", OBK)])

            def dmm(pp):
                P.op("pe", I("matmul", bank(DBK), ones_b, s2[pp % 2], start=(pp == 0), stop=(pp == NP // 2 - 1)),
                     reads=[("s2", pp % 2), "ones"], writes=[("ps", DBK)])

            scores(0)
            scores(1)
            for p in range(NP):
                b0, b1 = SCP[p % 2]
                pt = PT[p % NPT]
                P.op("act", I("activation", pt, ps_t[:, b0:b0 + 2, :], AF.Exp, scale=ATTN_SCALE),
                     reads=[("ps", b0), ("ps", b1)], writes=[("PT", p % NPT)])
                if p >= 1:
                    pv(p - 1)
                if p >= 2 and p % 2 == 0:
                    dmm(p // 2 - 1)
                if p + 2 < NP:
                    scores(p + 2)
                P.op("dve", I("tensor_tensor", s1[p % 2], pt[:, 0, :], pt[:, 1, :], ALU.add),
                     reads=[("PT", p % NPT)], writes=[("s1", p % 2)])
                if p % 2 == 1:
                    P.op("dve", I("tensor_tensor", s2[(p // 2) % 2], s1[0], s1[1], ALU.add),
                         reads=[("s1", 0), ("s1", 1)], writes=[("s2", (p // 2) % 2)])
                if p == NP // 2 and h + 1 < NH:
                    qproj(h + 1)
            pv(NP - 1)
            dmm(NP // 2 - 1)
            P.op("act", I("activation", osb, bank(OBK), AF.Copy), reads=[("ps", OBK)], writes=["osb"])
            P.op("act", I("activation", dsb, bank(DBK), AF.Copy), reads=[("ps", DBK)], writes=["dsb"])
            P.op("dve", I("reciprocal", dsb, dsb), reads=["dsb"], writes=["dsb"])
            P.op("dve", I("tensor_tensor", og[:, h, :], osb, dsb, ALU.mult),
                 reads=["osb", "dsb"], writes=[("og", h)])
        AR.off = mark

    def phase3(seg, tile, og):
        mark = AR.off
        HW = TOK + 32
        hT = AR.alloc(KC * HW, BF16).rearrange("p (c t) -> p c t", c=KC)
        xb = [AR.alloc(D, F32) for _ in range(2)]
        hsb = [AR.alloc(D, BF16) for _ in range(1)]
        ssb = [AR.alloc(1, F32) for _ in range(2)]
        glu = [AR.alloc(TOK + 32, BF16) for _ in range(2)]
        dg = AR.alloc(CONVK * 128, BF16).rearrange("p (j m) -> p j m", j=CONVK)
        sigh = AR.alloc(32, F32)
        aT = AR.alloc(KC * TOK, BF16).rearrange("p (c t) -> p c t", c=KC)
        asq = [AR.alloc(TOK, BF16) for _ in range(2)]
        mean_b = AR.alloc(TOK, F32)
        rstd_b = AR.alloc(TOK, F32)
        t1 = [AR.alloc(TOK, F32) for _ in range(2)]
        t2 = [AR.alloc(TOK, F32) for _ in range(2)]
        mT = AR.alloc(KC * TOK, BF16).rearrange("p (c t) -> p c t", c=KC)
        xT = AR.alloc(KC * TOK, F32).rearrange("p (c t) -> p c t", c=KC)
        pTb = AR.alloc(2 * TOK, BF16).rearrange("p (c t) -> p c t", c=2)
        pb = AR.alloc(256, F32)
        pbb = AR.alloc(256, BF16)
        NWA = 4
        wA = [AR.alloc(KC * 128, BF16).rearrange("p (c n) -> p c n", c=KC) for _ in range(NWA)]
        wpe = [AR.alloc(256, BF16).rearrange("p (c n) -> p c n", c=2) for _ in range(2)]
        own0 = tile * TOK
        wctr = [0]

        def loadA(blk):
            j = wctr[0] % NWA
            wctr[0] += 1
            w = wA[j]
            if seg == 0 and blk >= (tile + 1) * (160 // NT):
                P.dma("pool", w.rearrange("p c n -> p (c n)"), wA_d[blk], writes=[("wA", j)])
            else:
                P.dma("sp", w.rearrange("p c n -> p (c n)"), wAbf_d[blk], reads=[("wAbf", blk)], writes=[("wA", j)])
            return w, ("wA", j)

        for sub in range(5):
            i = sub % 2
            rows = 128 if sub < 4 else 32
            if sub < 4:
                r0 = own0 + sub * 128
                P.dma("sp", xb[i], xseq[seg][r0:r0 + 128, :], writes=[f"xb{i}"])
            else:
                P.dma("sp", xb[i][0:32, :], xhalo[seg, tile], writes=[f"xb{i}"])
            rms_tokens(xb[i], hsb[0], ssb[i], rows, f"xb{i}", "hs0")
            transpose_to_hT(hsb[0], rows, "hs0", hT, sub * 128, ("hT", sub))
        hk = [("hT", s) for s in range(4)]
        hka = hk + [("hT", 4)]

        s1b, s2b = next_bank(1), next_bank(1)
        reserved.update((s1b, s2b))
        sighs = [AR.alloc(128, F32)[0:32, :] for _ in range(2)]
        ghs = [AR.alloc(128, BF16)[0:32, :] for _ in range(2)]

        def s2a_front(c):
            g = glu[c % 2]
            gk = ("glu", c % 2)
            sh = sighs[c % 2]
            gh = ghs[c % 2]
            wv, kv = loadA(2 * c)
            bv = next_bank(1)
            mm_group(bank(bv), [(wv[:, k, :], hT[:, k, 0:TOK]) for k in range(KC)], hk + [kv], [("ps", bv)])
            bvh = next_bank(1)
            mm_group(bank(bvh)[0:32, 0:128], [(hT[:, k, TOK:TOK + 32], wv[:, k, :]) for k in range(KC)], hka + [kv], [("ps", bvh)])
            wg, kg = loadA(2 * c + 1)
            bg = next_bank(1)
            mm_group(bank(bg), [(wg[:, k, :], hT[:, k, 0:TOK]) for k in range(KC)], hk + [kg], [("ps", bg)])
            mm_group(bank(bvh)[0:32, 128:256], [(hT[:, k, TOK:TOK + 32], wg[:, k, :]) for k in range(KC)], hka + [kg], [("ps", bvh)])
            sg = t1[c % 2]
            P.op("act", I("activation", sg, bank(bg), AF.Sigmoid), reads=[("ps", bg)], writes=[("t1", c % 2)])
            P.op("act", I("activation", sh, bank(bvh)[0:32, 128:256], AF.Sigmoid), reads=[("ps", bvh)], writes=[("sigh", c % 2)])
            P.op("dve", I("tensor_tensor", g[:, 15:15 + TOK], bank(bv), sg, ALU.mult),
                 reads=[("ps", bv), ("t1", c % 2)], writes=[gk])
            P.op("dve", I("tensor_tensor", gh, bank(bvh)[0:32, 0:128], sh, ALU.mult),
                 reads=[("ps", bvh), ("sigh", c % 2)], writes=[("gh", c % 2)])
            bt = next_bank(1)
            ptv = bank(bt).bitcast(BF16)
            P.op("pe", I("transpose", ptv[:, 0:32], gh, ident_b[0:32, 0:32]), reads=[("gh", c % 2), "ident_b"], writes=[("ps", bt)])
            P.op("act", I("activation", g[:, 0:15], ptv[:, 0:15], AF.Copy), reads=[("ps", bt)], writes=[gk])
            P.op("act", I("activation", g[:, 15 + TOK:30 + TOK], ptv[:, 16:31], AF.Copy), reads=[("ps", bt)], writes=[gk])

        def s2a_conv(c):
            g = glu[c % 2]
            gk = ("glu", c % 2)
            cw = V_CW + c * CONVK
            P.op("dve", I("tensor_tensor", dg, ident_b.unsqueeze(1).broadcast_to([128, CONVK, 128]),
                          vecs[:, cw:cw + CONVK].unsqueeze(2).broadcast_to([128, CONVK, 128]), ALU.mult),
                 reads=["ident_b", "vecs"], writes=["dg"])
            bc = next_bank(1)
            mm_group(bank(bc), [(dg[:, j, :], g[:, j:j + TOK]) for j in range(CONVK)], [gk, "dg"], [("ps", bc)])
            P.op("act", I("activation", aT[:, c, :], bank(bc), AF.Identity, bias=vecs[:, V_CB + c:V_CB + c + 1], scale=1.0),
                 reads=[("ps", bc), "vecs"], writes=[("aT", c)])
            q = asq[c % 2]
            P.op("act", I("activation", q, aT[:, c, :], AF.Square), reads=[("aT", c)], writes=[("asq", c % 2)])

        def s2a_stats(c):
            q = asq[c % 2]
            P.op("pe", I("matmul", bank(s1b), ones_b, aT[:, c, :], start=(c == 0), stop=(c == KC - 1)),
                 reads=[("aT", c), "ones"], writes=[("ps", s1b)])
            P.op("pe", I("matmul", bank(s2b), ones_b, q, start=(c == 0), stop=(c == KC - 1)),
                 reads=[("asq", c % 2), "ones"], writes=[("ps", s2b)])

        s2a_front(0)
        for c in range(KC):
            if c + 1 < KC:
                s2a_front(c + 1)
            s2a_conv(c)
            if c >= 1:
                s2a_stats(c - 1)
        s2a_stats(KC - 1)
        reserved.difference_update((s1b, s2b))
        P.op("dve", I("tensor_scalar", mean_b, bank(s1b), 1.0 / D, None, ALU.mult), reads=[("ps", s1b)], writes=["mean_b"])
        P.op("dve", I("tensor_tensor", t1[0], mean_b, mean_b, ALU.mult), reads=["mean_b"], writes=[("t1", 0)])
        P.op("dve", I("scalar_tensor_tensor", rstd_b, bank(s2b), 1.0 / D, t1[0], ALU.mult, ALU.subtract),
             reads=[("ps", s2b), ("t1", 0)], writes=["rstd_b"])
        rsqrt_op(rstd_b, rstd_b, C_EPS, ["rstd_b"], "rstd_b")
        for c in range(KC):
            wz, kz = loadA(32 + c)
            bz = next_bank(1)
            mm_group(bank(bz), [(wz[:, k, :], hT[:, k, 0:TOK]) for k in range(KC)], hk + [kz], [("ps", bz)])
            u, v = t1[c % 2], t2[c % 2]
            P.op("dve", I("tensor_tensor", u, aT[:, c, :], mean_b, ALU.subtract),
                 reads=[("aT", c), "mean_b"], writes=[("t1", c % 2)])
            P.op("dve", I("tensor_tensor", u, u, rstd_b, ALU.mult), reads=[("t1", c % 2), "rstd_b"], writes=[("t1", c % 2)])
            P.op("act", I("activation", u, u, AF.Silu, bias=vecs[:, V_LNB + c:V_LNB + c + 1],
                          scale=vecs[:, V_LNG + c:V_LNG + c + 1]),
                 reads=[("t1", c % 2), "vecs"], writes=[("t1", c % 2)])
            P.op("act", I("activation", v, bank(bz), AF.Silu), reads=[("ps", bz)], writes=[("t2", c % 2)])
            P.op("dve", I("tensor_tensor", aT[:, c, :], u, v, ALU.mult),
                 reads=[("t1", c % 2), ("t2", c % 2)], writes=[("aT", c)])
        for c in range(KC):
            wz, kz = loadA(48 + c)
            bz = next_bank(1)
            mm_group(bank(bz), [(wz[:, k, :], hT[:, k, 0:TOK]) for k in range(KC)], hk + [kz], [("ps", bz)])
            v = t2[c % 2]
            P.op("act", I("activation", v, bank(bz), AF.Silu), reads=[("ps", bz)], writes=[("t2", c % 2)])
            P.op("dve", I("tensor_tensor", og[:, c, :], og[:, c, :], v, ALU.mult),
                 reads=[("og", c), ("t2", c % 2)], writes=[("og", c)])
        ak_all = [("aT", c) for c in range(KC)]
        ok_all = [("og", c) for c in range(KC)]
        for d in range(KC):
            w1, k1 = loadA(64 + 4 * d)
            b1 = next_bank(1)
            mm_group(bank(b1), [(w1[:, k, :], aT[:, k, :]) for k in range(KC)], ak_all + [k1], [("ps", b1)])
            w2, k2 = loadA(64 + 4 * d + 1)
            b2 = next_bank(1)
            mm_group(bank(b2), [(w2[:, k, :], hT[:, k, 0:TOK]) for k in range(KC)], hk + [k2], [("ps", b2)])
            u = t1[d % 2]
            P.op("act", I("activation", u, bank(b2), AF.Sigmoid), reads=[("ps", b2)], writes=[("t1", d % 2)])
            P.op("dve", I("tensor_tensor", u, bank(b1), u, ALU.mult), reads=[("ps", b1), ("t1", d % 2)], writes=[("t1", d % 2)])
            w3, k3 = loadA(64 + 4 * d + 2)
            b3 = next_bank(1)
            mm_group(bank(b3), [(w3[:, k, :], og[:, k, :]) for k in range(KC)], ok_all + [k3], [("ps", b3)])
            w4, k4 = loadA(64 + 4 * d + 3)
            b4 = next_bank(1)
            mm_group(bank(b4), [(w4[:, k, :], hT[:, k, 0:TOK]) for k in range(KC)], hk + [k4], [("ps", b4)])
            v = t2[d % 2]
            P.op("act", I("activation", v, bank(b4), AF.Sigmoid), reads=[("ps", b4)], writes=[("t2", d % 2)])
            P.op("dve", I("tensor_tensor", v, bank(b3), v, ALU.mult), reads=[("ps", b3), ("t2", d % 2)], writes=[("t2", d % 2)])
            P.op("dve", I("tensor_tensor", mT[:, d, :], u, v, ALU.add),
                 reads=[("t1", d % 2), ("t2", d % 2)], writes=[("mT", d)])
        for sub in range(4):
            i = sub % 2
            r0 = own0 + sub * 128
            P.dma("sp", xb[i], xseq[seg][r0:r0 + 128, :], writes=[f"xb{i}"])
            for cg in range(4):
                b = next_bank(1)
                insts = [I("transpose", bank(b)[:, cc * 128:(cc + 1) * 128],
                           xb[i][:, (cg * 4 + cc) * 128:(cg * 4 + cc + 1) * 128], ident_f) for cc in range(4)]
                P.op("pe", insts, reads=[f"xb{i}", "ident_f"], writes=[("ps", b)])
                evac_copy(xT[:, cg * 4:(cg + 1) * 4, sub * 128:(sub + 1) * 128],
                          bank(b).rearrange("p (c t) -> p c t", c=4), [("ps", b)], [("xT", cg, sub)])
        mk_all = [("mT", c) for c in range(KC)]
        x1b = hT
        for d in range(KC):
            w1, k1 = loadA(128 + d)
            b1 = next_bank(1)
            mm_group(bank(b1), [(w1[:, k, :], mT[:, k, :]) for k in range(KC)], mk_all + [k1], [("ps", b1)])
            xk = [("xT", d // 4, s) for s in range(4)]
            P.op("dve", I("tensor_tensor", xT[:, d, :], xT[:, d, :], bank(b1), ALU.add),
                 reads=[("ps", b1)] + xk, writes=[("x1", d)])
            P.op("act", I("activation", x1b[:, d, 0:TOK], xT[:, d, :], AF.Copy),
                 reads=[("x1", d)], writes=[("x1b", d)] + hk)
        for sub in range(4):
            r0 = own0 + sub * 128
            P.dma("sp", pb, pown[seg, r0:r0 + 128, :], writes=["pb"])
            P.op("dve", I("tensor_copy", pbb, pb), reads=["pb"], writes=["pbb"])
            b = next_bank(1)
            pv = bank(b).bitcast(BF16)
            P.op("pe", [I("transpose", pv[:, 0:128], pbb[:, 0:128], ident_b),
                        I("transpose", pv[:, 128:256], pbb[:, 128:256], ident_b)],
                 reads=["pbb", "ident_b"], writes=[("ps", b)])
            evac_copy(pTb[:, :, sub * 128:(sub + 1) * 128], pv[:, 0:256].rearrange("p (c t) -> p c t", c=2),
                      [("ps", b)], [("pT", sub)])
        pk = [("pT", s) for s in range(4)]
        xbk = [("x1b", c) for c in range(KC)]
        ssb_ = next_bank(1)
        reserved.add(ssb_)
        for d in range(KC):
            w1, k1 = loadA(144 + d)
            b1 = next_bank(1)
            mm_group(bank(b1), [(w1[:, k, :], x1b[:, k, 0:TOK]) for k in range(KC)], xbk + [k1], [("ps", b1)])
            wp = wpe[d % 2]
            P.dma("pool", wp.rearrange("p c n -> p (c n)"), wpe_d[d], writes=[("wpe", d % 2)])
            b2 = next_bank(1)
            mm_group(bank(b2), [(wp[:, k, :], pTb[:, k, :]) for k in range(2)], pk + [("wpe", d % 2)], [("ps", b2)])
            u = t1[d % 2]
            P.op("act", I("activation", u, bank(b1), AF.Sigmoid), reads=[("ps", b1)], writes=[("t1", d % 2)])
            P.op("dve", I("tensor_tensor", u, bank(b2), u, ALU.mult), reads=[("ps", b2), ("t1", d % 2)], writes=[("t1", d % 2)])
            P.op("dve", I("tensor_tensor", xT[:, d, :], xT[:, d, :], u, ALU.add),
                 reads=[("x1", d), ("t1", d % 2)], writes=[("x2", d)])
            q = asq[d % 2]
            P.op("act", I("activation", q, xT[:, d, :], AF.Square), reads=[("x2", d)], writes=[("asq", d % 2)])
            P.op("pe", I("matmul", bank(ssb_), ones_b, q, start=(d == 0), stop=(d == KC - 1)),
                 reads=[("asq", d % 2), "ones"], writes=[("ps", ssb_)])
        reserved.discard(ssb_)
        rsqrt_op(rstd_b, bank(ssb_), C_DEPS, [("ps", ssb_)], "rstd_b")
        for d in range(KC):
            P.op("dve", I("scalar_tensor_tensor", xT[:, d, :], xT[:, d, :], gfin_s[:, d:d + 1], rstd_b, ALU.mult, ALU.mult),
                 reads=[("x2", d), "rstd_b", "gfin_s"], writes=[("yT", d)])
        for sub in range(4):
            i = sub % 2
            r0 = own0 + sub * 128
            for cg in range(4):
                b = next_bank(1)
                insts = [I("transpose", bank(b)[:, cc * 128:(cc + 1) * 128],
                           xT[:, cg * 4 + cc, sub * 128:(sub + 1) * 128], ident_f) for cc in range(4)]
                P.op("pe", insts, reads=[("yT", cg * 4 + cc) for cc in range(4)] + ["ident_f"], writes=[("ps", b)])
                evac_copy(xb[i][:, cg * 512:(cg + 1) * 512], bank(b), [("ps", b)], [(f"yb{i}", cg)])
            P.dma("sp", yout[seg][r0:r0 + 128, :], xb[i], reads=[(f"yb{i}", cg) for cg in range(4)],
                  writes=[("yout", seg, r0)])
        AR.off = mark

    for seg in range(2):
        phase1(seg)
        P.barrier()
        AR.off = persist_off
        og = AR.alloc(KC * TOK, BF16).rearrange("p (c t) -> p c t", c=KC)
        for tile in range(NT):
            phase2(seg, tile, og)
            P.barrier()
            phase3(seg, tile, og)
            P.barrier()

    csem = {}
    slotsem = {}
    for e in ENGS:
        csem[e] = es.enter_context(nc.semaphore(f"c_{e}"))
        slotsem[e] = [es.enter_context(nc.semaphore(f"s_{e}{i}")) for i in range(NSLOT)]
    P.finalize(csem, slotsem)
    print("PROG stats", P.stats, "arena peak words", AR.peak, flush=True)
    with nc.Block() as block:
        @block.tensor
        def _(eng):
            P.emit("pe", eng)

        @block.scalar
        def _(eng):
            P.emit("act", eng)

        @block.vector
        def _(eng):
            P.emit("dve", eng)

        @block.gpsimd
        def _(eng):
            P.emit("pool", eng)

        @block.sync
        def _(eng):
            P.emit("sp", eng)
    es.close()
    return nc


IN_OFF = dict(cv=0, cg=2048, zc=4096, cq=6144, ckv=6656, kr=7168, za=7232, gc=9280, ga=11328)


def a_blocks(w):
    K, N = w.shape
    t = w.reshape(K // 128, 128, N // 128, 128).transpose(2, 1, 0, 3)
    return np.ascontiguousarray(t).reshape(N // 128, 128, (K // 128) * 128)


def rope_tables(S):
    inv = (1.0 / (np.float32(10000.0) ** (np.arange(0, 64, 2, dtype=np.float32) / np.float32(64)))).astype(np.float32)
    ang = np.arange(S, dtype=np.float32)[:, None] * inv[None, :]
    c = np.cos(ang).astype(np.float32).T
    s = np.sin(ang).astype(np.float32).T
    return np.concatenate([c, c], 0), np.concatenate([-s, s], 0)


def fm(v, n):
    return np.ascontiguousarray(v.reshape(n, 128).T)


def prepare_inputs(cfg, inp):
    NT, OWN = cfg.NT, cfg.OWN
    w_in = inp["w_in"][0]
    sw = np.concatenate([np.arange(32, 64), np.arange(0, 32)])
    kr = w_in[:, IN_OFF["kr"]:IN_OFF["kr"] + 64]
    lat = np.concatenate([w_in[:, IN_OFF["ckv"]:IN_OFF["ckv"] + 512], kr, kr[:, sw]], 1)
    w_lat = np.ascontiguousarray(lat.reshape(KC, 128, 640).transpose(1, 0, 2)).reshape(128, KC * 640)
    w_cq = a_blocks(w_in[:, IN_OFF["cq"]:IN_OFF["cq"] + 512])
    ukv = inp["w_ukv"][0].reshape(512, NH, 256)
    w_uk = np.ascontiguousarray(ukv[:, :, 0:128].reshape(4, 128, 2048).transpose(1, 0, 2)).reshape(128, 8192)
    w_uv = np.ascontiguousarray(ukv[:, :, 128:256].reshape(4, 128, 2048).transpose(1, 0, 2)).reshape(128, 8192)
    uq = inp["w_uq"][0].reshape(512, NH, 192)
    uq = np.concatenate([uq, uq[:, :, 128 + sw]], 2)
    w_uq = np.ascontiguousarray(uq.reshape(4, 128, NH, 256).transpose(2, 1, 0, 3)).reshape(NH, 128, 1024)

    def wb(name):
        return a_blocks(w_in[:, IN_OFF[name]:IN_OFF[name] + 2048])
    cv, cg, zc, za, gc, ga = (wb(n) for n in ("cv", "cg", "zc", "za", "gc", "ga"))
    wco, wo, wout, wpg = (a_blocks(inp[n][0]) for n in ("w_conv_out", "w_o", "w_out", "w_pg"))
    blocks = []
    for c in range(16):
        blocks += [cv[c], cg[c]]
    blocks += list(zc) + list(za)
    for d in range(16):
        blocks += [wco[d], gc[d], wo[d], ga[d]]
    blocks += list(wout) + list(wpg)
    wA = np.stack(blocks, 0)
    w_pe = a_blocks(inp["w_pe"][0])
    vecs = np.zeros((128, NV), np.float32)
    vecs[:, V_GFIN:V_GFIN + 16] = fm(inp["g_final"], 16)
    vecs[:, V_QN:V_QN + 4] = fm(inp["q_norm"][0], 4)
    vecs[:, V_KVN:V_KVN + 4] = fm(inp["kv_norm"][0], 4)
    vecs[:, V_CB:V_CB + 16] = fm(inp["conv_b"][0], 16)
    vecs[:, V_LNG:V_LNG + 16] = fm(inp["ln_g"][0], 16)
    vecs[:, V_LNB:V_LNB + 16] = fm(inp["ln_b"][0], 16)
    cw = inp["conv_w"][0][:, 0, :]
    vecs[:, V_CW:] = np.ascontiguousarray(cw.T.reshape(16, 128, CONVK).transpose(1, 0, 2)).reshape(128, 16 * CONVK)
    shared = dict(
        gpre_b=np.ascontiguousarray(np.broadcast_to(inp["g_pre"][0][None, :], (128, D))),
        vecs=vecs, ident=np.eye(128, dtype=np.float32), w_lat=w_lat, w_cq=w_cq, w_uk=w_uk, w_uv=w_uv,
        w_uq=w_uq, wA=wA, w_pe=w_pe)
    seqs = [inp["x_sample"][0], None]
    pseq = [inp["p_sample"][0, 0], None]
    tabs = [rope_tables(cfg.S[0]), rope_tables(cfg.S[1])]
    maps = []
    for core in range(NCORES):
        b, j = core // 4, core % 4
        starts = [core * OWN, j * OWN]
        xs = [seqs[0], inp["x_prompt"][b]]
        ps = [pseq[0], inp["p_prompt"][0, b]]
        m = dict(shared)
        halo = np.zeros((2, NT, 32, D), np.float32)
        pown = np.zeros((2, OWN, 256), np.float32)
        for s in range(2):
            S = cfg.S[s]
            st = starts[s]
            m[f"xseq{s}"] = np.ascontiguousarray(np.roll(xs[s], -st, axis=0))
            m[f"cos{s}"] = np.ascontiguousarray(np.roll(tabs[s][0], -st, axis=1))
            m[f"sin{s}"] = np.ascontiguousarray(np.roll(tabs[s][1], -st, axis=1))
            pown[s] = ps[s][st:st + OWN]
            xp = np.zeros((S + 32, D), np.float32)
            xp[16:16 + S] = xs[s]
            for t in range(NT):
                a = st + t * TOK
                halo[s, t, 0:15] = xp[16 + a - 15:16 + a]
                halo[s, t, 16:31] = xp[16 + a + TOK:16 + a + TOK + 15]
        m["xhalo"] = halo
        m["pown"] = pown
        maps.append(m)
    return maps


_CACHE = {}


def run(cfg, inputs):
    key = (cfg.NT, cfg.S)
    if key not in _CACHE:
        _CACHE[key] = build_program(cfg)
    nc = _CACHE[key]
    inp = {k: np.asarray(v, dtype=np.float32) for k, v in inputs.items()}
    maps = prepare_inputs(cfg, inp)
    res = run_bass_kernel_spmd(nc, maps, core_ids=list(range(NCORES)))
    OWN = cfg.OWN
    ys = np.concatenate([res.results[c]["y0"] for c in range(NCORES)], 0)[None]
    yp = np.stack([np.concatenate([res.results[b * 4 + j]["y1"] for j in range(4)], 0) for b in range(2)], 0)
    return yp.astype(np.float32), ys.astype(np.float32)


def kernel(**inputs):
    return run(Cfg(2, 8192, 4096), inputs)
```

```python
import math
from contextlib import ExitStack

import numpy as np
import concourse.bass as bass
import concourse.mybir as mybir
from concourse.bass_utils import run_bass_kernel_spmd

F32 = mybir.dt.float32
BF16 = mybir.dt.bfloat16
ALU = mybir.AluOpType
AF = mybir.ActivationFunctionType

D = 2048
KC = 16
TOK = 512
NH = 16
EPS = 1e-6
ATTN_SCALE = 1.0 / math.sqrt(192.0)
CONVK = 31
NCORES = 8

V_GFIN, V_QN, V_KVN, V_CB, V_LNG, V_LNB, V_CW = 0, 16, 20, 24, 40, 56, 72
NV = 72 + 16 * CONVK

ENGS = ("pe", "act", "dve", "pool", "sp")
NSLOT = 8


class Op:
    __slots__ = ("eng", "insts", "dma", "deps", "need_inc", "sem", "val", "waits")

    def __init__(self, eng, insts, dma):
        self.eng, self.insts, self.dma = eng, insts, dma
        self.deps = set()
        self.need_inc = dma
        self.sem = None
        self.val = 0
        self.waits = []


def I(name, *a, **kw):
    return (name, a, kw)


class Prog:
    def __init__(self):
        self.by_eng = {e: [] for e in ENGS}
        self.last_write = {}
        self.reads = {}
        self.last_real = {}
        self.dma_pending = []

    def op(self, eng, insts, reads=(), writes=(), dma=False):
        if isinstance(insts, tuple):
            insts = [insts]
        o = Op(eng, insts, dma)
        deps = []
        for k in reads:
            w = self.last_write.get(k)
            if w is not None:
                deps.append((w, 0))
        for k in writes:
            w = self.last_write.get(k)
            if w is not None:
                deps.append((w, 1))
            r = self.reads.get(k)
            if r:
                for x in r[0].values():
                    deps.append((x, 2))
                for x in r[1]:
                    deps.append((x, 2))
        for p, kind in deps:
            if p.eng == eng and not p.dma and not dma:
                if eng == "pe" or kind != 0:
                    continue
            o.deps.add(p)
        for k in reads:
            r = self.reads.setdefault(k, ({}, []))
            if dma:
                r[1].append(o)
            else:
                r[0][eng] = o
        for k in writes:
            self.last_write[k] = o
            self.reads[k] = ({}, [])
        self.by_eng[eng].append(o)
        self.last_real[eng] = o
        if dma:
            self.dma_pending.append(o)
        return o

    def dma(self, eng, out, in_, reads=(), writes=()):
        return self.op(eng, I("dma_start", out=out, in_=in_), reads, writes, dma=True)

    def barrier(self):
        targets = list(self.last_real.values()) + self.dma_pending
        for e in ENGS:
            o = Op(e, None, False)
            o.need_inc = False
            for t in targets:
                if t.eng == e and not t.dma:
                    continue
                o.deps.add(t)
            self.by_eng[e].append(o)
        self.dma_pending = []
        self.last_write = {}
        self.reads = {}

    def finalize(self, csem, slotsem):
        for e in ENGS:
            for o in self.by_eng[e]:
                for p in o.deps:
                    p.need_inc = True
        for e in ENGS:
            cnt = 0
            nd = 0
            for o in self.by_eng[e]:
                if o.insts is None:
                    continue
                if o.dma:
                    o.sem = slotsem[e][nd % NSLOT]
                    o.val = 16 * (nd // NSLOT + 1)
                    nd += 1
                elif o.need_inc:
                    cnt += 1
                    o.sem = csem[e]
                    o.val = cnt
        nwait = 0
        for e in ENGS:
            known = {}
            for o in self.by_eng[e]:
                need = {}
                for p in o.deps:
                    k = id(p.sem)
                    if k not in need or need[k][1] < p.val:
                        need[k] = (p.sem, p.val)
                if o.dma and o.val > 16:
                    k = id(o.sem)
                    v = o.val - 16
                    if k not in need or need[k][1] < v:
                        need[k] = (o.sem, v)
                for k, (s, v) in need.items():
                    if known.get(k, 0) >= v:
                        continue
                    known[k] = v
                    o.waits.append((s, v))
                    nwait += 1
        self.stats = {e: len(self.by_eng[e]) for e in ENGS}
        self.stats["waits"] = nwait

    def emit(self, e, eng):
        for o in self.by_eng[e]:
            for s, v in o.waits:
                eng.wait_ge(s, v)
            if o.insts is not None:
                inst = None
                for name, a, kw in o.insts:
                    inst = getattr(eng, name)(*a, **kw)
                if o.need_inc:
                    inst.then_inc(o.sem, 16 if o.dma else 1)


class Arena:
    def __init__(self, t, nwords):
        self.t = t
        self.nwords = nwords
        self.off = 0
        self.peak = 0

    def alloc(self, nelem, dt):
        nb = nelem * (2 if dt == BF16 else 4)
        nw = (nb + 63) // 64 * 16
        assert self.off + nw <= self.nwords, f"SBUF arena overflow {self.off + nw} > {self.nwords}"
        ap = self.t[:, self.off:self.off + nw]
        self.off += nw
        self.peak = max(self.peak, self.off)
        if dt == BF16:
            ap = ap.bitcast(BF16)
        return ap[:, 0:nelem]


class Cfg:
    def __init__(self, nt=2, s0=8192, s1=4096):
        self.NT = nt
        self.S = (s0, s1)
        self.OWN = nt * TOK


def build_program(cfg):
    NT = cfg.NT
    OWN = cfg.OWN
    nc = bass.Bass("TRN2", target_bir_lowering=False)

    def din(name, shape, dt=F32):
        return nc.dram_tensor(name, list(shape), dt, kind="ExternalInput").ap()

    xseq = [din(f"xseq{s}", (cfg.S[s], D)) for s in range(2)]
    cosd = [din(f"cos{s}", (64, cfg.S[s])) for s in range(2)]
    sind = [din(f"sin{s}", (64, cfg.S[s])) for s in range(2)]
    xhalo = din("xhalo", (2, NT, 32, D))
    pown = din("pown", (2, OWN, 256))
    gpre_d = din("gpre_b", (128, D))
    vecs_d = din("vecs", (128, NV))
    ident_d = din("ident", (128, 128))
    wlat_d = din("w_lat", (128, KC * 640))
    wcq_d = din("w_cq", (4, 128, KC * 128))
    wuk_d = din("w_uk", (128, 4 * 2048))
    wuv_d = din("w_uv", (128, 4 * 2048))
    wuq_d = din("w_uq", (NH, 128, 4 * 256))
    wA_d = din("wA", (160, 128, KC * 128))
    wpe_d = din("w_pe", (16, 128, 2 * 128))
    yout = [nc.dram_tensor(f"y{s}", [OWN, D], F32, kind="ExternalOutput").ap() for s in range(2)]
    Kd = [nc.dram_tensor(f"kscr{s}", [NH, 128, cfg.S[s]], BF16, kind="Internal").ap() for s in range(2)]
    Vd = [nc.dram_tensor(f"vscr{s}", [cfg.S[s] // 128, 128, D], BF16, kind="Internal").ap() for s in range(2)]

    wAbf_d = nc.dram_tensor("wA_bf", [160, 128, KC * 128], BF16, kind="Internal").ap()

    P = Prog()
    es = ExitStack()
    NW = 53000
    arena_t = es.enter_context(nc.sbuf_tensor("arena", [128, NW], F32))
    ps_t = es.enter_context(nc.psum_tensor("ps", [128, 8, 512], F32))
    AR = Arena(arena_t, NW)

    def bank(b):
        return ps_t[:, b, :]

    bank_ctr = [0]
    reserved = set()

    def next_bank(n=1):
        while True:
            b = bank_ctr[0]
            if b + n > 8:
                b = 0
            bank_ctr[0] = (b + n) % 8
            if all((b + j) not in reserved for j in range(n)):
                return b

    ident_f = AR.alloc(128, F32)
    ident_b = AR.alloc(128, BF16)
    ones_b = AR.alloc(128, BF16)
    vecs = AR.alloc(NV, F32)
    gpre = AR.alloc(D, F32)
    kvn_s = AR.alloc(4, F32)
    qn_s = AR.alloc(4, F32)
    gfin_s = AR.alloc(16, F32)
    cst = AR.alloc(4, F32)
    C_DEPS, C_512EPS, C_EPS = cst[:, 0:1], cst[:, 1:2], cst[:, 2:3]
    P.op("dve", I("memset", C_DEPS, D * EPS), writes=["cst"])
    P.op("dve", I("memset", C_512EPS, 512 * EPS), writes=["cst"])
    P.op("dve", I("memset", C_EPS, EPS), writes=["cst"])

    def rsqrt_op(out, in_, cbias, reads, key):
        P.op("act", I("activation", out, in_, AF.Sqrt, bias=cbias, scale=1.0), reads=reads, writes=[key])
        P.op("dve", I("reciprocal", out, out), reads=[key], writes=[key])

    P.dma("sp", ident_f, ident_d, writes=["ident_f"])
    P.dma("pool", ident_b, ident_d, writes=["ident_b"])
    P.dma("sp", vecs, vecs_d, writes=["vecs"])
    P.dma("sp", gpre, gpre_d, writes=["gpre"])
    P.op("dve", I("memset", ones_b, 1.0), writes=["ones"])
    P.op("dve", I("tensor_scalar", gpre, gpre, math.sqrt(D), None, ALU.mult), reads=["gpre"], writes=["gpre"])
    P.op("dve", I("tensor_scalar", kvn_s, vecs[:, V_KVN:V_KVN + 4], math.sqrt(512.0), None, ALU.mult),
         reads=["vecs"], writes=["kvn_s"])
    P.op("dve", I("tensor_scalar", qn_s, vecs[:, V_QN:V_QN + 4], math.sqrt(512.0), None, ALU.mult),
         reads=["vecs"], writes=["qn_s"])
    P.op("dve", I("tensor_scalar", gfin_s, vecs[:, V_GFIN:V_GFIN + 16], math.sqrt(D), None, ALU.mult),
         reads=["vecs"], writes=["gfin_s"])
    P.barrier()

    evac_rr = [0]

    def evac_copy(out, in_, reads, writes):
        evac_rr[0] ^= 1
        if evac_rr[0]:
            return P.op("act", I("activation", out, in_, AF.Copy), reads=reads, writes=writes)
        return P.op("dve", I("tensor_copy", out, in_), reads=reads, writes=writes)

    def mm_group(out, pairs, reads, writes):
        n = len(pairs)
        insts = [I("matmul", out, l, r, start=(i == 0), stop=(i == n - 1)) for i, (l, r) in enumerate(pairs)]
        return P.op("pe", insts, reads=reads, writes=writes)

    def rms_tokens(xb, hsb, ssb, rows, keyx, keyh):
        P.op("act", I("activation", hsb[0:rows, :], xb[0:rows, :], AF.Square, accum_out=ssb[0:rows, :]),
             reads=[keyx], writes=[keyh, keyh + "ss"])
        rsqrt_op(ssb[0:rows, :], ssb[0:rows, :], C_DEPS[0:rows, :], [keyh + "ss"], keyh + "ss")
        P.op("dve", I("scalar_tensor_tensor", hsb[0:rows, :], xb[0:rows, :], ssb[0:rows, 0:1], gpre[0:rows, :],
                      ALU.mult, ALU.mult),
             reads=[keyx, keyh + "ss", "gpre"], writes=[keyh])

    def transpose_to_hT(hsb, rows, keyh, hT3, col0, keyout):
        b = next_bank(2)
        pv = ps_t[:, b:b + 2, :].bitcast(BF16).rearrange("p b (c t) -> p (b c) t", t=128)
        insts = [I("transpose", pv[:, c, 0:rows], hsb[0:rows, c * 128:(c + 1) * 128], ident_b[0:rows, 0:rows])
                 for c in range(KC)]
        P.op("pe", insts, reads=[keyh, "ident_b"], writes=[("ps", b), ("ps", b + 1)])
        evac_copy(hT3[:, :, col0:col0 + rows], pv[:, :, 0:rows], [("ps", b), ("ps", b + 1)], [keyout])

    featnorm_rstd = AR.alloc(TOK, F32)
    rope_t1 = AR.alloc(TOK, F32)[0:64, :]
    rope_t2 = AR.alloc(TOK, F32)[0:64, :]
    cqn = AR.alloc(4 * OWN, BF16).rearrange("p (c t) -> p c t", c=4)
    krope = AR.alloc(max(cfg.S), BF16)
    P.op("dve", I("memset", krope[64:128, :], 0.0), writes=["krope_pad"])
    P.barrier()
    persist_off = AR.off

    def featnorm(srcbanks, nchunk, gain_s, out3, col0, tag, sqring):
        sb = next_bank(1)
        reserved.add(sb)
        for c in range(nchunk):
            sq = sqring[c % len(sqring)]
            kq = ("fnsq", c % len(sqring))
            P.op("act", I("activation", sq, bank(srcbanks[c]), AF.Square), reads=[("ps", srcbanks[c])], writes=[kq])
            P.op("pe", I("matmul", bank(sb), ones_b, sq, start=(c == 0), stop=(c == nchunk - 1)),
                 reads=[kq, "ones"], writes=[("ps", sb)])
        reserved.discard(sb)
        rb = featnorm_rstd
        assert nchunk == 4
        rsqrt_op(rb, bank(sb), C_512EPS, [("ps", sb)], "fn_rstd")
        for c in range(nchunk):
            P.op("dve", I("scalar_tensor_tensor", out3[:, c, col0:col0 + TOK], bank(srcbanks[c]), gain_s[:, c:c + 1], rb,
                          ALU.mult, ALU.mult),
                 reads=[("ps", srcbanks[c]), "fn_rstd"], writes=[(tag + "out", c)])

    def rope(pb_a, pb_b, cosT, sinT, out, reads, writes):
        P.op("dve", I("tensor_tensor", rope_t1, bank(pb_a)[0:64, :], cosT, ALU.mult),
             reads=[("ps", pb_a)] + reads, writes=["rope_t1"])
        P.op("dve", I("tensor_tensor", rope_t2, bank(pb_b)[0:64, :], sinT, ALU.mult),
             reads=[("ps", pb_b)] + reads, writes=["rope_t2"])
        P.op("dve", I("tensor_tensor", out, rope_t1, rope_t2, ALU.add),
             reads=["rope_t1", "rope_t2"], writes=writes)

    def phase1(seg):
        S = cfg.S[seg]
        NTL = S // TOK
        AR.off = persist_off
        wlat = AR.alloc(KC * 640, BF16).rearrange("p (c n) -> p c n", c=KC)
        wuk = AR.alloc(4 * 2048, BF16).rearrange("p (c n) -> p c n", c=4)
        wuv = AR.alloc(4 * 2048, BF16).rearrange("p (c n) -> p c n", c=4)
        xb = [AR.alloc(D, F32) for _ in range(2)]
        hsb = [AR.alloc(D, BF16) for _ in range(4)]
        ssb = [AR.alloc(1, F32) for _ in range(4)]
        hTs = [AR.alloc(KC * TOK, BF16).rearrange("p (c t) -> p c t", c=KC) for _ in range(2)]
        ckvns = [AR.alloc(4 * TOK, BF16).rearrange("p (c t) -> p c t", c=4) for _ in range(2)]
        sqring = [AR.alloc(TOK, BF16) for _ in range(2)]
        cosTs = [AR.alloc(TOK, F32)[0:64, :] for _ in range(2)]
        sinTs = [AR.alloc(TOK, F32)[0:64, :] for _ in range(2)]
        kst = [AR.alloc(4 * TOK, BF16).rearrange("p (h t) -> p h t", h=4) for _ in range(2)]
        vst = [AR.alloc(D, BF16) for _ in range(2)]
        wcq = [AR.alloc(KC * 128, BF16).rearrange("p (c n) -> p c n", c=KC) for _ in range(2)]

        P.dma("pool", wlat.rearrange("p c n -> p (c n)"), wlat_d, writes=["wlat"])
        P.dma("pool", wuk.rearrange("p c n -> p (c n)"), wuk_d, writes=["wuk"])
        P.dma("pool", wuv.rearrange("p c n -> p (c n)"), wuv_d, writes=["wuv"])

        def front_a(tt, subs=(0, 1, 2, 3)):
            par = tt % 2
            t0 = tt * TOK
            if 0 in subs:
                P.dma("sp", cosTs[par], cosd[seg][:, t0:t0 + TOK], writes=[("cosT", par)])
                P.dma("sp", sinTs[par], sind[seg][:, t0:t0 + TOK], writes=[("sinT", par)])
            for sub in subs:
                i = sub % 2
                r0 = t0 + sub * 128
                P.dma("sp", xb[i], xseq[seg][r0:r0 + 128, :], writes=[f"xb{i}"])
                rms_tokens(xb[i], hsb[sub], ssb[sub], 128, f"xb{i}", f"hs{sub}")

        def front_b(tt):
            par = tt % 2
            for sub in range(4):
                transpose_to_hT(hsb[sub], 128, f"hs{sub}", hTs[par], sub * 128, ("hT", par, sub))

        def mid(tt):
            par = tt % 2
            t0 = tt * TOK
            hT = hTs[par]
            ckvn = ckvns[par]
            hkeys = [("hT", par, s_) for s_ in range(4)]
            cb = [next_bank(1) for _ in range(4)]
            for c in range(4):
                mm_group(bank(cb[c]), [(wlat[:, k, c * 128:(c + 1) * 128], hT[:, k, :]) for k in range(KC)],
                         hkeys + ["wlat"], [("ps", cb[c])])
            kb = next_bank(1)
            mm_group(bank(kb)[0:64, :], [(wlat[:, k, 512:576], hT[:, k, :]) for k in range(KC)], hkeys + ["wlat"], [("ps", kb)])
            kb2 = next_bank(1)
            mm_group(bank(kb2)[0:64, :], [(wlat[:, k, 576:640], hT[:, k, :]) for k in range(KC)], hkeys + ["wlat"], [("ps", kb2)])
            featnorm(cb, 4, kvn_s, ckvn, 0, f"kvn{par}", sqring)
            rope(kb, kb2, cosTs[par], sinTs[par], krope[0:64, t0:t0 + TOK], [("cosT", par), ("sinT", par)], [("krope", tt)])
            if tt < NT:
                qb = []
                for c in range(4):
                    w = wcq[c % 2]
                    P.dma("pool", w.rearrange("p c n -> p (c n)"), wcq_d[c], writes=[("wcq", c % 2)])
                    b = next_bank(1)
                    qb.append(b)
                    mm_group(bank(b), [(w[:, k, :], hT[:, k, :]) for k in range(KC)], hkeys + [("wcq", c % 2)], [("ps", b)])
                featnorm(qb, 4, qn_s, cqn, t0, "qn", sqring)

        def back_k(tt):
            par = tt % 2
            t0 = tt * TOK
            ckvn = ckvns[par]
            ck = [(f"kvn{par}out", c) for c in range(4)]
            for hg in range(4):
                ks = kst[hg % 2]
                for hh in range(4):
                    h = hg * 4 + hh
                    b = next_bank(1)
                    mm_group(bank(b), [(wuk[:, c, h * 128:(h + 1) * 128], ckvn[:, c, :]) for c in range(4)],
                             ck + ["wuk"], [("ps", b)])
                    evac_copy(ks[:, hh, :], bank(b), [("ps", b)], [("kst", hg % 2, hh)])
                P.dma("sp", Kd[seg][hg * 4:(hg + 1) * 4, :, t0:t0 + TOK].rearrange("h p t -> p h t"), ks,
                      reads=[("kst", hg % 2, x) for x in range(4)], writes=[("Kd", seg, hg, tt)])

        def back_v(tt):
            par = tt % 2
            ckvn = ckvns[par]
            ck = [(f"kvn{par}out", c) for c in range(4)]
            for sub in range(4):
                vs = vst[sub % 2]
                for g in range(4):
                    b = next_bank(1)
                    mm_group(bank(b), [(ckvn[:, c, sub * 128:(sub + 1) * 128], wuv[:, c, g * 512:(g + 1) * 512])
                                       for c in range(4)], ck + ["wuv"], [("ps", b)])
                    evac_copy(vs[:, g * 512:(g + 1) * 512], bank(b), [("ps", b)], [("vst", sub % 2, g)])
                P.dma("sp", Vd[seg][tt * 4 + sub], vs,
                      reads=[("vst", sub % 2, x) for x in range(4)], writes=[("Vd", seg, tt, sub)])

        front_a(0)
        front_b(0)
        for tt in range(NTL):
            if tt + 1 < NTL:
                front_a(tt + 1, (0, 1))
            mid(tt)
            if tt + 1 < NTL:
                front_a(tt + 1, (2, 3))
            if tt >= 1:
                back_k(tt - 1)
            if tt + 1 < NTL:
                front_b(tt + 1)
            if tt >= 1:
                back_v(tt - 1)
        back_k(NTL - 1)
        back_v(NTL - 1)

    def phase2(seg, tile, og):
        S = cfg.S[seg]
        NKT = S // 128
        NP = NKT // 2
        assert NP % 2 == 0 and NP >= 4
        q0 = tile * TOK
        mark = AR.off
        Kb = [AR.alloc(S, BF16) for _ in range(2)]
        Vb = [AR.alloc(S, BF16).rearrange("p (k d) -> p k d", d=128) for _ in range(2)]
        wq = [AR.alloc(4 * 256, BF16).rearrange("p (c n) -> p c n", c=4) for _ in range(2)]
        qn = [AR.alloc(TOK, BF16) for _ in range(2)]
        qr = [AR.alloc(TOK, BF16) for _ in range(2)]
        for qq in range(2):
            P.op("dve", I("memset", qr[qq][64:128, :], 0.0), writes=[("qrpad", qq)])
        NPT = 4
        PT = [AR.alloc(2 * TOK, BF16).rearrange("p (b t) -> p b t", b=2) for _ in range(NPT)]
        s1 = [AR.alloc(TOK, BF16) for _ in range(2)]
        s2 = [AR.alloc(TOK, BF16) for _ in range(2)]
        osb = AR.alloc(TOK, F32)
        dsb = AR.alloc(TOK, F32)
        cosq = AR.alloc(TOK, F32)[0:64, :]
        sinq = AR.alloc(TOK, F32)[0:64, :]
        SCP, OBK, DBK, QA, QB = ((0, 1), (2, 3)), 4, 5, 6, 7

        P.dma("sp", cosq, cosd[seg][:, q0:q0 + TOK], writes=["cosq"])
        P.dma("sp", sinq, sind[seg][:, q0:q0 + TOK], writes=["sinq"])
        VSTEP = 16
        cqk = [("qnout", c) for c in range(4)]

        def load_head(h):
            i = h % 2
            P.dma("pool", wq[i].rearrange("p c n -> p (c n)"), wuq_d[h], writes=[("wq", i)])
            P.dma("sp", Kb[i], Kd[seg][h], writes=[("Kb", i)])
            for k0 in range(0, NKT, VSTEP):
                P.dma("sp", Vb[i][:, k0:k0 + VSTEP, :],
                      Vd[seg][k0:k0 + VSTEP, :, h * 128:(h + 1) * 128].rearrange("k p d -> p k d"),
                      writes=[("Vb", i, k0)])

        def qproj(h):
            i = h % 2
            mm_group(bank(QA), [(wq[i][:, c, 0:128], cqn[:, c, q0:q0 + TOK]) for c in range(4)],
                     [("wq", i)] + cqk, [("ps", QA)])
            P.op("dve", I("tensor_copy", qn[i], bank(QA)), reads=[("ps", QA)], writes=[("qn", i)])
            mm_group(bank(QB)[0:64, :], [(wq[i][:, c, 128:192], cqn[:, c, q0:q0 + TOK]) for c in range(4)],
                     [("wq", i)] + cqk, [("ps", QB)])
            mm_group(bank(QA)[0:64, :], [(wq[i][:, c, 192:256], cqn[:, c, q0:q0 + TOK]) for c in range(4)],
                     [("wq", i)] + cqk, [("ps", QA)])
            rope(QB, QA, cosq, sinq, qr[i][0:64, :], ["cosq", "sinq"], [("qr", i)])

        load_head(0)
        qproj(0)
        for h in range(NH):
            i = h % 2
            if h + 1 < NH:
                load_head(h + 1)
            if seg == 0:
                per = 160 // NT
                ph = (per + NH - 1) // NH
                for blk in range(tile * per + h * ph, min((tile + 1) * per, tile * per + (h + 1) * ph)):
                    P.dma("pool", wAbf_d[blk], wA_d[blk], writes=[("wAbf", blk)])
            vkeys = [("Vb", i, k0) for k0 in range(0, NKT, VSTEP)]

            def scores(p, i=i):
                b0, b1 = SCP[p % 2]
                insts = []
                for b, kt in ((b0, 2 * p), (b1, 2 * p + 1)):
                    insts.append(I("matmul", bank(b), Kb[i][:, kt * 128:(kt + 1) * 128], qn[i], start=True, stop=False))
                    insts.append(I("matmul", bank(b), krope[:, kt * 128:(kt + 1) * 128], qr[i], start=False, stop=True))
                P.op("pe", insts, reads=[("Kb", i), ("qn", i), ("qr", i), ("qrpad", i)], writes=[("ps", b0), ("ps", b1)])

            def pv(p, i=i, vkeys=vkeys):
                pt = PT[p % NPT]
                P.op("pe", [I("matmul", bank(OBK), Vb[i][:, 2 * p, :], pt[:, 0, :], start=(p == 0), stop=False),
                            I("matmul", bank(OBK), Vb[i][:, 2 * p + 1, :], pt[:, 1, :], start=False, stop=(p == NP - 1))],
                     reads=[("PT", p % NPT)] + vkeys, writes=[("ps", OBK)])

            def dmm(pp):
                P.op("pe", I("matmul", bank(DBK), ones_b, s2[pp % 2], start=(pp == 0), stop=(pp == NP // 2 - 1)),
                     reads=[("s2", pp % 2), "ones"], writes=[("ps", DBK)])

            scores(0)
            scores(1)
            for p in range(NP):
                b0, b1 = SCP[p % 2]
                pt = PT[p % NPT]
                P.op("act", I("activation", pt, ps_t[:, b0:b0 + 2, :], AF.Exp, scale=ATTN_SCALE),
                     reads=[("ps", b0), ("ps", b1)], writes=[("PT", p % NPT)])
                if p >= 1:
                    pv(p - 1)
                if p >= 2 and p % 2 == 0:
                    dmm(p // 2 - 1)
                if p + 2 < NP:
                    scores(p + 2)
                P.op("dve", I("tensor_tensor", s1[p % 2], pt[:, 0, :], pt[:, 1, :], ALU.add),
                     reads=[("PT", p % NPT)], writes=[("s1", p % 2)])
                if p % 2 == 1:
                    P.op("dve", I("tensor_tensor", s2[(p // 2) % 2], s1[0], s1[1], ALU.add),
                         reads=[("s1", 0), ("s1", 1)], writes=[("s2", (p // 2) % 2)])
                if p == NP // 2 and h + 1 < NH:
                    qproj(h + 1)
            pv(NP - 1)
            dmm(NP // 2 - 1)
            P.op("act", I("activation", osb, bank(OBK), AF.Copy), reads=[("ps", OBK)], writes=["osb"])
            P.op("act", I("activation", dsb, bank(DBK), AF.Copy), reads=[("ps", DBK)], writes=["dsb"])
            P.op("dve", I("reciprocal", dsb, dsb), reads=["dsb"], writes=["dsb"])
            P.op("dve", I("tensor_tensor", og[:, h, :], osb, dsb, ALU.mult),
                 reads=["osb", "dsb"], writes=[("og", h)])
        AR.off = mark

    def phase3(seg, tile, og):
        mark = AR.off
        HW = TOK + 32
        hT = AR.alloc(KC * HW, BF16).rearrange("p (c t) -> p c t", c=KC)
        xb = [AR.alloc(D, F32) for _ in range(2)]
        hsb = [AR.alloc(D, BF16) for _ in range(1)]
        ssb = [AR.alloc(1, F32) for _ in range(2)]
        glu = [AR.alloc(TOK + 32, BF16) for _ in range(2)]
        dg = AR.alloc(CONVK * 128, BF16).rearrange("p (j m) -> p j m", j=CONVK)
        sigh = AR.alloc(32, F32)
        aT = AR.alloc(KC * TOK, BF16).rearrange("p (c t) -> p c t", c=KC)
        asq = [AR.alloc(TOK, BF16) for _ in range(2)]
        mean_b = AR.alloc(TOK, F32)
        rstd_b = AR.alloc(TOK, F32)
        t1 = [AR.alloc(TOK, F32) for _ in range(2)]
        t2 = [AR.alloc(TOK, F32) for _ in range(2)]
        mT = AR.alloc(KC * TOK, BF16).rearrange("p (c t) -> p c t", c=KC)
        xT = AR.alloc(KC * TOK, F32).rearrange("p (c t) -> p c t", c=KC)
        pTb = AR.alloc(2 * TOK, BF16).rearrange("p (c t) -> p c t", c=2)
        pb = AR.alloc(256, F32)
        pbb = AR.alloc(256, BF16)
        NWA = 4
        wA = [AR.alloc(KC * 128, BF16).rearrange("p (c n) -> p c n", c=KC) for _ in range(NWA)]
        wpe = [AR.alloc(256, BF16).rearrange("p (c n) -> p c n", c=2) for _ in range(2)]
        own0 = tile * TOK
        wctr = [0]

        def loadA(blk):
            j = wctr[0] % NWA
            wctr[0] += 1
            w = wA[j]
            if seg == 0 and blk >= (tile + 1) * (160 // NT):
                P.dma("pool", w.rearrange("p c n -> p (c n)"), wA_d[blk], writes=[("wA", j)])
            else:
                P.dma("sp", w.rearrange("p c n -> p (c n)"), wAbf_d[blk], reads=[("wAbf", blk)], writes=[("wA", j)])
            return w, ("wA", j)

        for sub in range(5):
            i = sub % 2
            rows = 128 if sub < 4 else 32
            if sub < 4:
                r0 = own0 + sub * 128
                P.dma("sp", xb[i], xseq[seg][r0:r0 + 128, :], writes=[f"xb{i}"])
            else:
                P.dma("sp", xb[i][0:32, :], xhalo[seg, tile], writes=[f"xb{i}"])
            rms_tokens(xb[i], hsb[0], ssb[i], rows, f"xb{i}", "hs0")
            transpose_to_hT(hsb[0], rows, "hs0", hT, sub * 128, ("hT", sub))
        hk = [("hT", s) for s in range(4)]
        hka = hk + [("hT", 4)]

        s1b, s2b = next_bank(1), next_bank(1)
        reserved.update((s1b, s2b))
        sighs = [AR.alloc(128, F32)[0:32, :] for _ in range(2)]
        ghs = [AR.alloc(128, BF16)[0:32, :] for _ in range(2)]

        def s2a_front(c):
            g = glu[c % 2]
            gk = ("glu", c % 2)
            sh = sighs[c % 2]
            gh = ghs[c % 2]
            wv, kv = loadA(2 * c)
            bv = next_bank(1)
            mm_group(bank(bv), [(wv[:, k, :], hT[:, k, 0:TOK]) for k in range(KC)], hk + [kv], [("ps", bv)])
            bvh = next_bank(1)
            mm_group(bank(bvh)[0:32, 0:128], [(hT[:, k, TOK:TOK + 32], wv[:, k, :]) for k in range(KC)], hka + [kv], [("ps", bvh)])
            wg, kg = loadA(2 * c + 1)
            bg = next_bank(1)
            mm_group(bank(bg), [(wg[:, k, :], hT[:, k, 0:TOK]) for k in range(KC)], hk + [kg], [("ps", bg)])
            mm_group(bank(bvh)[0:32, 128:256], [(hT[:, k, TOK:TOK + 32], wg[:, k, :]) for k in range(KC)], hka + [kg], [("ps", bvh)])
            sg = t1[c % 2]
            P.op("act", I("activation", sg, bank(bg), AF.Sigmoid), reads=[("ps", bg)], writes=[("t1", c % 2)])
            P.op("act", I("activation", sh, bank(bvh)[0:32, 128:256], AF.Sigmoid), reads=[("ps", bvh)], writes=[("sigh", c % 2)])
            P.op("dve", I("tensor_tensor", g[:, 15:15 + TOK], bank(bv), sg, ALU.mult),
                 reads=[("ps", bv), ("t1", c % 2)], writes=[gk])
            P.op("dve", I("tensor_tensor", gh, bank(bvh)[0:32, 0:128], sh, ALU.mult),
                 reads=[("ps", bvh), ("sigh", c % 2)], writes=[("gh", c % 2)])
            bt = next_bank(1)
            ptv = bank(bt).bitcast(BF16)
            P.op("pe", I("transpose", ptv[:, 0:32], gh, ident_b[0:32, 0:32]), reads=[("gh", c % 2), "ident_b"], writes=[("ps", bt)])
            P.op("act", I("activation", g[:, 0:15], ptv[:, 0:15], AF.Copy), reads=[("ps", bt)], writes=[gk])
            P.op("act", I("activation", g[:, 15 + TOK:30 + TOK], ptv[:, 16:31], AF.Copy), reads=[("ps", bt)], writes=[gk])

        def s2a_conv(c):
            g = glu[c % 2]
            gk = ("glu", c % 2)
            cw = V_CW + c * CONVK
            P.op("dve", I("tensor_tensor", dg, ident_b.unsqueeze(1).broadcast_to([128, CONVK, 128]),
                          vecs[:, cw:cw + CONVK].unsqueeze(2).broadcast_to([128, CONVK, 128]), ALU.mult),
                 reads=["ident_b", "vecs"], writes=["dg"])
            bc = next_bank(1)
            mm_group(bank(bc), [(dg[:, j, :], g[:, j:j + TOK]) for j in range(CONVK)], [gk, "dg"], [("ps", bc)])
            P.op("act", I("activation", aT[:, c, :], bank(bc), AF.Identity, bias=vecs[:, V_CB + c:V_CB + c + 1], scale=1.0),
                 reads=[("ps", bc), "vecs"], writes=[("aT", c)])
            q = asq[c % 2]
            P.op("act", I("activation", q, aT[:, c, :], AF.Square), reads=[("aT", c)], writes=[("asq", c % 2)])

        def s2a_stats(c):
            q = asq[c % 2]
            P.op("pe", I("matmul", bank(s1b), ones_b, aT[:, c, :], start=(c == 0), stop=(c == KC - 1)),
                 reads=[("aT", c), "ones"], writes=[("ps", s1b)])
            P.op("pe", I("matmul", bank(s2b), ones_b, q, start=(c == 0), stop=(c == KC - 1)),
                 reads=[("asq", c % 2), "ones"], writes=[("ps", s2b)])

        s2a_front(0)
        for c in range(KC):
            if c + 1 < KC:
                s2a_front(c + 1)
            s2a_conv(c)
            if c >= 1:
                s2a_stats(c - 1)
        s2a_stats(KC - 1)
        reserved.difference_update((s1b, s2b))
        P.op("dve", I("tensor_scalar", mean_b, bank(s1b), 1.0 / D, None, ALU.mult), reads=[("ps", s1b)], writes=["mean_b"])
        P.op("dve", I("tensor_tensor", t1[0], mean_b, mean_b, ALU.mult), reads=["mean_b"], writes=[("t1", 0)])
        P.op("dve", I("scalar_tensor_tensor", rstd_b, bank(s2b), 1.0 / D, t1[0], ALU.mult, ALU.subtract),
             reads=[("ps", s2b), ("t1", 0)], writes=["rstd_b"])
        rsqrt_op(rstd_b, rstd_b, C_EPS, ["rstd_b"], "rstd_b")
        for c in range(KC):
            wz, kz = loadA(32 + c)
            bz = next_bank(1)
            mm_group(bank(bz), [(wz[:, k, :], hT[:, k, 0:TOK]) for k in range(KC)], hk + [kz], [("ps", bz)])
            u, v = t1[c % 2], t2[c % 2]
            P.op("dve", I("tensor_tensor", u, aT[:, c, :], mean_b, ALU.subtract),
                 reads=[("aT", c), "mean_b"], writes=[("t1", c % 2)])
            P.op("dve", I("tensor_tensor", u, u, rstd_b, ALU.mult), reads=[("t1", c % 2), "rstd_b"], writes=[("t1", c % 2)])
            P.op("act", I("activation", u, u, AF.Silu, bias=vecs[:, V_LNB + c:V_LNB + c + 1],
                          scale=vecs[:, V_LNG + c:V_LNG + c + 1]),
                 reads=[("t1", c % 2), "vecs"], writes=[("t1", c % 2)])
            P.op("act", I("activation", v, bank(bz), AF.Silu), reads=[("ps", bz)], writes=[("t2", c % 2)])
            P.op("dve", I("tensor_tensor", aT[:, c, :], u, v, ALU.mult),
                 reads=[("t1", c % 2), ("t2", c % 2)], writes=[("aT", c)])
        for c in range(KC):
            wz, kz = loadA(48 + c)
            bz = next_bank(1)
            mm_group(bank(bz), [(wz[:, k, :], hT[:, k, 0:TOK]) for k in range(KC)], hk + [kz], [("ps", bz)])
            v = t2[c % 2]
            P.op("act", I("activation", v, bank(bz), AF.Silu), reads=[("ps", bz)], writes=[("t2", c % 2)])
            P.op("dve", I("tensor_tensor", og[:, c, :], og[:, c, :], v, ALU.mult),
                 reads=[("og", c), ("t2", c % 2)], writes=[("og", c)])
        ak_all = [("aT", c) for c in range(KC)]
        ok_all = [("og", c) for c in range(KC)]
        for d in range(KC):
            w1, k1 = loadA(64 + 4 * d)
            b1 = next_bank(1)
            mm_group(bank(b1), [(w1[:, k, :], aT[:, k, :]) for k in range(KC)], ak_all + [k1], [("ps", b1)])
            w2, k2 = loadA(64 + 4 * d + 1)
            b2 = next_bank(1)
            mm_group(bank(b2), [(w2[:, k, :], hT[:, k, 0:TOK]) for k in range(KC)], hk + [k2], [("ps", b2)])
            u = t1[d % 2]
            P.op("act", I("activation", u, bank(b2), AF.Sigmoid), reads=[("ps", b2)], writes=[("t1", d % 2)])
            P.op("dve", I("tensor_tensor", u, bank(b1), u, ALU.mult), reads=[("ps", b1), ("t1", d % 2)], writes=[("t1", d % 2)])
            w3, k3 = loadA(64 + 4 * d + 2)
            b3 = next_bank(1)
            mm_group(bank(b3), [(w3[:, k, :], og[:, k, :]) for k in range(KC)], ok_all + [k3], [("ps", b3)])
            w4, k4 = loadA(64 + 4 * d + 3)
            b4 = next_bank(1)
            mm_group(bank(b4), [(w4[:, k, :], hT[:, k, 0:TOK]) for k in range(KC)], hk + [k4], [("ps", b4)])
            v = t2[d % 2]
            P.op("act", I("activation", v, bank(b4), AF.Sigmoid), reads=[("ps", b4)], writes=[("t2", d % 2)])
            P.op("dve", I("tensor_tensor", v, bank(b3), v, ALU.mult), reads=[("ps", b3), ("t2", d % 2)], writes=[("t2", d % 2)])
            P.op("dve", I("tensor_tensor", mT[:, d, :], u, v, ALU.add),
                 reads=[("t1", d % 2), ("t2", d % 2)], writes=[("mT", d)])
        for sub in range(4):
            i = sub % 2
            r0 = own0 + sub * 128
            P.dma("sp", xb[i], xseq[seg][r0:r0 + 128, :], writes=[f"xb{i}"])
            for cg in range(4):
                b = next_bank(1)
                insts = [I("transpose", bank(b)[:, cc * 128:(cc + 1) * 128],
                           xb[i][:, (cg * 4 + cc) * 128:(cg * 4 + cc + 1) * 128], ident_f) for cc in range(4)]
                P.op("pe", insts, reads=[f"xb{i}", "ident_f"], writes=[("ps", b)])
                evac_copy(xT[:, cg * 4:(cg + 1) * 4, sub * 128:(sub + 1) * 128],
                          bank(b).rearrange("p (c t) -> p c t", c=4), [("ps", b)], [("xT", cg, sub)])
        mk_all = [("mT", c) for c in range(KC)]
        x1b = hT
        for d in range(KC):
            w1, k1 = loadA(128 + d)
            b1 = next_bank(1)
            mm_group(bank(b1), [(w1[:, k, :], mT[:, k, :]) for k in range(KC)], mk_all + [k1], [("ps", b1)])
            xk = [("xT", d // 4, s) for s in range(4)]
            P.op("dve", I("tensor_tensor", xT[:, d, :], xT[:, d, :], bank(b1), ALU.add),
                 reads=[("ps", b1)] + xk, writes=[("x1", d)])
            P.op("act", I("activation", x1b[:, d, 0:TOK], xT[:, d, :], AF.Copy),
                 reads=[("x1", d)], writes=[("x1b", d)] + hk)
        for sub in range(4):
            r0 = own0 + sub * 128
            P.dma("sp", pb, pown[seg, r0:r0 + 128, :], writes=["pb"])
            P.op("dve", I("tensor_copy", pbb, pb), reads=["pb"], writes=["pbb"])
            b = next_bank(1)
            pv = bank(b).bitcast(BF16)
            P.op("pe", [I("transpose", pv[:, 0:128], pbb[:, 0:128], ident_b),
                        I("transpose", pv[:, 128:256], pbb[:, 128:256], ident_b)],
                 reads=["pbb", "ident_b"], writes=[("ps", b)])
            evac_copy(pTb[:, :, sub * 128:(sub + 1) * 128], pv[:, 0:256].rearrange("p (c t) -> p c t", c=2),
                      [("ps", b)], [("pT", sub)])
        pk = [("pT", s) for s in range(4)]
        xbk = [("x1b", c) for c in range(KC)]
        ssb_ = next_bank(1)
        reserved.add(ssb_)
        for d in range(KC):
            w1, k1 = loadA(144 + d)
            b1 = next_bank(1)
            mm_group(bank(b1), [(w1[:, k, :], x1b[:, k, 0:TOK]) for k in range(KC)], xbk + [k1], [("ps", b1)])
            wp = wpe[d % 2]
            P.dma("pool", wp.rearrange("p c n -> p (c n)"), wpe_d[d], writes=[("wpe", d % 2)])
            b2 = next_bank(1)
            mm_group(bank(b2), [(wp[:, k, :], pTb[:, k, :]) for k in range(2)], pk + [("wpe", d % 2)], [("ps", b2)])
            u = t1[d % 2]
            P.op("act", I("activation", u, bank(b1), AF.Sigmoid), reads=[("ps", b1)], writes=[("t1", d % 2)])
            P.op("dve", I("tensor_tensor", u, bank(b2), u, ALU.mult), reads=[("ps", b2), ("t1", d % 2)], writes=[("t1", d % 2)])
            P.op("dve", I("tensor_tensor", xT[:, d, :], xT[:, d, :], u, ALU.add),
                 reads=[("x1", d), ("t1", d % 2)], writes=[("x2", d)])
            q = asq[d % 2]
            P.op("act", I("activation", q, xT[:, d, :], AF.Square), reads=[("x2", d)], writes=[("asq", d % 2)])
            P.op("pe", I("matmul", bank(ssb_), ones_b, q, start=(d == 0), stop=(d == KC - 1)),
                 reads=[("asq", d % 2), "ones"], writes=[("ps", ssb_)])
        reserved.discard(ssb_)
        rsqrt_op(rstd_b, bank(ssb_), C_DEPS, [("ps", ssb_)], "rstd_b")
        for d in range(KC):
            P.op("dve", I("scalar_tensor_tensor", xT[:, d, :], xT[:, d, :], gfin_s[:, d:d + 1], rstd_b, ALU.mult, ALU.mult),
                 reads=[("x2", d), "rstd_b", "gfin_s"], writes=[("yT", d)])
        for sub in range(4):
            i = sub % 2
            r0 = own0 + sub * 128
            for cg in range(4):
                b = next_bank(1)
                insts = [I("transpose", bank(b)[:, cc * 128:(cc + 1) * 128],
                           xT[:, cg * 4 + cc, sub * 128:(sub + 1) * 128], ident_f) for cc in range(4)]
                P.op("pe", insts, reads=[("yT", cg * 4 + cc) for cc in range(4)] + ["ident_f"], writes=[("ps", b)])
                evac_copy(xb[i][:, cg * 512:(cg + 1) * 512], bank(b), [("ps", b)], [(f"yb{i}", cg)])
            P.dma("sp", yout[seg][r0:r0 + 128, :], xb[i], reads=[(f"yb{i}", cg) for cg in range(4)],
                  writes=[("yout", seg, r0)])
        AR.off = mark

    for seg in range(2):
        phase1(seg)
        P.barrier()
        AR.off = persist_off
        og = AR.alloc(KC * TOK, BF16).rearrange("p (c t) -> p c t", c=KC)
        for tile in range(NT):
            phase2(seg, tile, og)
            P.barrier()
            phase3(seg, tile, og)
            P.barrier()

    csem = {}
    slotsem = {}
    for e in ENGS:
        csem[e] = es.enter_context(nc.semaphore(f"c_{e}"))
        slotsem[e] = [es.enter_context(nc.semaphore(f"s_{e}{i}")) for i in range(NSLOT)]
    P.finalize(csem, slotsem)
    print("PROG stats", P.stats, "arena peak words", AR.peak, flush=True)
    with nc.Block() as block:
        @block.tensor
        def _(eng):
            P.emit("pe", eng)

        @block.scalar
        def _(eng):
            P.emit("act", eng)

        @block.vector
        def _(eng):
            P.emit("dve", eng)

        @block.gpsimd
        def _(eng):
            P.emit("pool", eng)

        @block.sync
        def _(eng):
            P.emit("sp", eng)
    es.close()
    return nc


IN_OFF = dict(cv=0, cg=2048, zc=4096, cq=6144, ckv=6656, kr=7168, za=7232, gc=9280, ga=11328)


def a_blocks(w):
    K, N = w.shape
    t = w.reshape(K // 128, 128, N // 128, 128).transpose(2, 1, 0, 3)
    return np.ascontiguousarray(t).reshape(N // 128, 128, (K // 128) * 128)


def rope_tables(S):
    inv = (1.0 / (np.float32(10000.0) ** (np.arange(0, 64, 2, dtype=np.float32) / np.float32(64)))).astype(np.float32)
    ang = np.arange(S, dtype=np.float32)[:, None] * inv[None, :]
    c = np.cos(ang).astype(np.float32).T
    s = np.sin(ang).astype(np.float32).T
    return np.concatenate([c, c], 0), np.concatenate([-s, s], 0)


def fm(v, n):
    return np.ascontiguousarray(v.reshape(n, 128).T)


def prepare_inputs(cfg, inp):
    NT, OWN = cfg.NT, cfg.OWN
    w_in = inp["w_in"][0]
    sw = np.concatenate([np.arange(32, 64), np.arange(0, 32)])
    kr = w_in[:, IN_OFF["kr"]:IN_OFF["kr"] + 64]
    lat = np.concatenate([w_in[:, IN_OFF["ckv"]:IN_OFF["ckv"] + 512], kr, kr[:, sw]], 1)
    w_lat = np.ascontiguousarray(lat.reshape(KC, 128, 640).transpose(1, 0, 2)).reshape(128, KC * 640)
    w_cq = a_blocks(w_in[:, IN_OFF["cq"]:IN_OFF["cq"] + 512])
    ukv = inp["w_ukv"][0].reshape(512, NH, 256)
    w_uk = np.ascontiguousarray(ukv[:, :, 0:128].reshape(4, 128, 2048).transpose(1, 0, 2)).reshape(128, 8192)
    w_uv = np.ascontiguousarray(ukv[:, :, 128:256].reshape(4, 128, 2048).transpose(1, 0, 2)).reshape(128, 8192)
    uq = inp["w_uq"][0].reshape(512, NH, 192)
    uq = np.concatenate([uq, uq[:, :, 128 + sw]], 2)
    w_uq = np.ascontiguousarray(uq.reshape(4, 128, NH, 256).transpose(2, 1, 0, 3)).reshape(NH, 128, 1024)

    def wb(name):
        return a_blocks(w_in[:, IN_OFF[name]:IN_OFF[name] + 2048])
    cv, cg, zc, za, gc, ga = (wb(n) for n in ("cv", "cg", "zc", "za", "gc", "ga"))
    wco, wo, wout, wpg = (a_blocks(inp[n][0]) for n in ("w_conv_out", "w_o", "w_out", "w_pg"))
    blocks = []
    for c in range(16):
        blocks += [cv[c], cg[c]]
    blocks += list(zc) + list(za)
    for d in range(16):
        blocks += [wco[d], gc[d], wo[d], ga[d]]
    blocks += list(wout) + list(wpg)
    wA = np.stack(blocks, 0)
    w_pe = a_blocks(inp["w_pe"][0])
    vecs = np.zeros((128, NV), np.float32)
    vecs[:, V_GFIN:V_GFIN + 16] = fm(inp["g_final"], 16)
    vecs[:, V_QN:V_QN + 4] = fm(inp["q_norm"][0], 4)
    vecs[:, V_KVN:V_KVN + 4] = fm(inp["kv_norm"][0], 4)
    vecs[:, V_CB:V_CB + 16] = fm(inp["conv_b"][0], 16)
    vecs[:, V_LNG:V_LNG + 16] = fm(inp["ln_g"][0], 16)
    vecs[:, V_LNB:V_LNB + 16] = fm(inp["ln_b"][0], 16)
    cw = inp["conv_w"][0][:, 0, :]
    vecs[:, V_CW:] = np.ascontiguousarray(cw.T.reshape(16, 128, CONVK).transpose(1, 0, 2)).reshape(128, 16 * CONVK)
    shared = dict(
        gpre_b=np.ascontiguousarray(np.broadcast_to(inp["g_pre"][0][None, :], (128, D))),
        vecs=vecs, ident=np.eye(128, dtype=np.float32), w_lat=w_lat, w_cq=w_cq, w_uk=w_uk, w_uv=w_uv,
        w_uq=w_uq, wA=wA, w_pe=w_pe)
    seqs = [inp["x_sample"][0], None]
    pseq = [inp["p_sample"][0, 0], None]
    tabs = [rope_tables(cfg.S[0]), rope_tables(cfg.S[1])]
    maps = []
    for core in range(NCORES):
        b, j = core // 4, core % 4
        starts = [core * OWN, j * OWN]
        xs = [seqs[0], inp["x_prompt"][b]]
        ps = [pseq[0], inp["p_prompt"][0, b]]
        m = dict(shared)
        halo = np.zeros((2, NT, 32, D), np.float32)
        pown = np.zeros((2, OWN, 256), np.float32)
        for s in range(2):
            S = cfg.S[s]
            st = starts[s]
            m[f"xseq{s}"] = np.ascontiguousarray(np.roll(xs[s], -st, axis=0))
            m[f"cos{s}"] = np.ascontiguousarray(np.roll(tabs[s][0], -st, axis=1))
            m[f"sin{s}"] = np.ascontiguousarray(np.roll(tabs[s][1], -st, axis=1))
            pown[s] = ps[s][st:st + OWN]
            xp = np.zeros((S + 32, D), np.float32)
            xp[16:16 + S] = xs[s]
            for t in range(NT):
                a = st + t * TOK
                halo[s, t, 0:15] = xp[16 + a - 15:16 + a]
                halo[s, t, 16:31] = xp[16 + a + TOK:16 + a + TOK + 15]
        m["xhalo"] = halo
        m["pown"] = pown
        maps.append(m)
    return maps


_CACHE = {}


def run(cfg, inputs):
    key = (cfg.NT, cfg.S)
    if key not in _CACHE:
        _CACHE[key] = build_program(cfg)
    nc = _CACHE[key]
    inp = {k: np.asarray(v, dtype=np.float32) for k, v in inputs.items()}
    maps = prepare_inputs(cfg, inp)
    res = run_bass_kernel_spmd(nc, maps, core_ids=list(range(NCORES)))
    OWN = cfg.OWN
    ys = np.concatenate([res.results[c]["y0"] for c in range(NCORES)], 0)[None]
    yp = np.stack([np.concatenate([res.results[b * 4 + j]["y1"] for j in range(4)], 0) for b in range(2)], 0)
    return yp.astype(np.float32), ys.astype(np.float32)


def kernel(**inputs):
    return run(Cfg(2, 8192, 4096), inputs)
```
